# Optimizing a Trainium2 kernel written in Bass

```python
import math
import jax
import jax.numpy as jnp
from jax import lax
import numpy as np

D_MODEL = 1024
BATCH = 8
SEQ = 4096
DEPTH = 2

HEAD_DIM = 64
D_MIX = D_MODEL
N_HEADS_TOTAL = D_MIX // HEAD_DIM
N_HEADS_SB = N_HEADS_TOTAL // 4
N_HEADS_FOX = N_HEADS_TOTAL // 4
N_HEADS_NSA = N_HEADS_TOTAL - N_HEADS_SB - N_HEADS_FOX
NSA_KV_GROUPS = 2
NSA_HPG = N_HEADS_NSA // NSA_KV_GROUPS
D_SB = N_HEADS_SB * HEAD_DIM
D_FOX = N_HEADS_FOX * HEAD_DIM
D_NSA = N_HEADS_NSA * HEAD_DIM
D_NSA_KV = NSA_KV_GROUPS * HEAD_DIM
CMP_BLOCK = 32
CMP_STRIDE = 16
CMP_HIDDEN = 256
SLC_BLOCK = 64
SLC_TOP = 16
WINDOW = 512
NSA_QBLOCK = 32
ATTN_QBLOCK = 128
REL_BUCKETS = 32
REL_MAX_DIST = 128
FORCE_SCORE = 1e4
RMS_EPS = 1e-6
IN_WIDTH = 4 * D_SB + 2 * D_NSA + 6 * D_NSA_KV + 3 * N_HEADS_NSA + 4 * D_FOX + N_HEADS_FOX

kernel_name = 'hybrid_sb_nsa_fox_parallel_heads'


def _rmsnorm(x, g):
    xf = x.astype(jnp.float32)
    y = xf * lax.rsqrt(jnp.mean(xf * xf, axis=-1, keepdims=True) + RMS_EPS)
    return (y * g.astype(jnp.float32)).astype(x.dtype)


def _heads(x, n):
    b, t, _ = x.shape
    return x.reshape(b, t, n, HEAD_DIM).transpose(0, 2, 1, 3)


def _merge(o):
    b, n, t, d = o.shape
    return o.transpose(0, 2, 1, 3).reshape(b, t, n * d)


def _split_proj(proj):
    widths = [D_SB, D_SB, D_SB, D_SB,
              D_NSA, D_NSA_KV, D_NSA_KV, D_NSA_KV, D_NSA_KV, D_NSA_KV, D_NSA_KV, 3 * N_HEADS_NSA, D_NSA,
              D_FOX, D_FOX, D_FOX, N_HEADS_FOX, D_FOX]
    idx, acc = [], 0
    for w in widths[:-1]:
        acc += w
        idx.append(acc)
    return jnp.split(proj, idx, axis=-1)


def _rel_bucket(dist):
    n = jnp.maximum(dist, 0)
    max_exact = REL_BUCKETS // 2
    nf = jnp.maximum(n, 1).astype(jnp.float32)
    large = max_exact + (jnp.log(nf / max_exact) / math.log(REL_MAX_DIST / max_exact)
                         * (REL_BUCKETS - max_exact)).astype(jnp.int32)
    large = jnp.minimum(large, REL_BUCKETS - 1)
    return jnp.where(n < max_exact, n, large)


def _masked_softmax(s, mask):
    p = jax.nn.softmax(jnp.where(mask, s, -1e30), axis=-1)
    return jnp.where(mask, p, 0.0)


def _stick_breaking(q, k, v):
    b, h, t, d = q.shape
    scale = d ** -0.5
    kpos = jnp.arange(t)

    def block(i):
        t0 = i * ATTN_QBLOCK
        qb = lax.dynamic_slice_in_dim(q, t0, ATTN_QBLOCK, axis=2)
        z = jnp.einsum('bhqd,bhkd->bhqk', qb, k).astype(jnp.float32) * scale
        qpos = t0 + jnp.arange(ATTN_QBLOCK)
        mask = kpos[None, :] < qpos[:, None]
        log1m = jnp.where(mask, -jax.nn.softplus(z), 0.0)
        cs = lax.cumsum(log1m, axis=3)
        log_w = jax.nn.log_sigmoid(z) + cs[..., -1:] - cs
        w = jnp.where(mask, jnp.exp(log_w), 0.0)
        return jnp.einsum('bhqk,bhkd->bhqd', w.astype(v.dtype), v)

    o = lax.map(block, jnp.arange(t // ATTN_QBLOCK))
    return o.transpose(1, 2, 0, 3, 4).reshape(b, h, t, d)


def _forgetting_attention(q, k, v, log_f):
    b, h, t, d = q.shape
    scale = d ** -0.5
    c = lax.cumsum(log_f, axis=2)
    kpos = jnp.arange(t)

    def block(i):
        t0 = i * ATTN_QBLOCK
        qb = lax.dynamic_slice_in_dim(q, t0, ATTN_QBLOCK, axis=2)
        cq = lax.dynamic_slice_in_dim(c, t0, ATTN_QBLOCK, axis=2)
        s = (jnp.einsum('bhqd,bhkd->bhqk', qb, k).astype(jnp.float32) * scale
             + cq[..., :, None] - c[..., None, :])
        qpos = t0 + jnp.arange(ATTN_QBLOCK)
        s = jnp.where(kpos[None, :] <= qpos[:, None], s, -jnp.inf)
        p = jax.nn.softmax(s, axis=-1)
        return jnp.einsum('bhqk,bhkd->bhqd', p.astype(v.dtype), v)

    o = lax.map(block, jnp.arange(t // ATTN_QBLOCK))
    return o.transpose(1, 2, 0, 3, 4).reshape(b, h, t, d)


def _compress(x, w1, b1, w2, pe):
    b, g, t, d = x.shape
    halves = x.reshape(b, g, t // CMP_STRIDE, CMP_STRIDE * d)
    w1a, w1b = w1[:CMP_STRIDE * d], w1[CMP_STRIDE * d:]
    h = halves[:, :, :-1] @ w1a + halves[:, :, 1:] @ w1b + (pe.reshape(-1) @ w1 + b1)
    return jax.nn.silu(h) @ w2


def _nsa(q, kc, vc, ks, vs, kw, vw, gate_logits, rel_table, w1, b1, w2, pe):
    b, hb, t, d = q.shape
    G, HPG = NSA_KV_GROUPS, NSA_HPG
    dt = q.dtype
    scale = d ** -0.5
    qg = q.reshape(b, G, HPG, t, d)
    k_cmp = _compress(kc, w1[0], b1[0], w2[0], pe[0])
    v_cmp = _compress(vc, w1[1], b1[1], w2[1], pe[1])
    n_cmp = k_cmp.shape[2]
    cmp_end = jnp.arange(n_cmp) * CMP_STRIDE + CMP_BLOCK - 1
    cmp_start = cmp_end - (CMP_BLOCK - 1)
    n_slc = t // SLC_BLOCK
    n_top = min(SLC_TOP, n_slc)
    slc_start = jnp.arange(n_slc) * SLC_BLOCK
    overlap = jnp.clip(jnp.minimum(cmp_end[:, None], slc_start[None, :] + SLC_BLOCK - 1)
                       - jnp.maximum(cmp_start[:, None], slc_start[None, :]) + 1, 0, None
                       ).astype(jnp.float32) / CMP_BLOCK
    ks_blk = ks.reshape(b, G, n_slc, SLC_BLOCK, d)
    vs_blk = vs.reshape(b, G, n_slc, SLC_BLOCK, d)
    kw_pad = jnp.pad(kw, ((0, 0), (0, 0), (WINDOW, 0), (0, 0)))
    vw_pad = jnp.pad(vw, ((0, 0), (0, 0), (WINDOW, 0), (0, 0)))
    table = rel_table.reshape(REL_BUCKETS, G, HPG)
    bi = jnp.arange(b)[:, None, None, None]
    gi = jnp.arange(G)[None, :, None, None]
    win_off = jnp.arange(WINDOW + NSA_QBLOCK) - WINDOW
    tok_in_blk = jnp.arange(SLC_BLOCK)
    blk_ids = jnp.arange(n_slc)

    def bias_2d(dist):
        return table[_rel_bucket(dist)].transpose(2, 3, 0, 1)

    def block(i):
        t0 = i * NSA_QBLOCK
        qpos = t0 + jnp.arange(NSA_QBLOCK)
        qb = lax.dynamic_slice_in_dim(qg, t0, NSA_QBLOCK, axis=3)
        s_c = (jnp.einsum('bghqd,bgcd->bghqc', qb, k_cmp).astype(jnp.float32) * scale
               + bias_2d(qpos[:, None] - cmp_end[None, :]))
        p_c = _masked_softmax(s_c, cmp_end[None, :] <= qpos[:, None])
        o_c = jnp.einsum('bghqc,bgcd->bghqd', p_c.astype(dt), v_cmp)
        imp = jnp.einsum('bghqc,cs->bgqs', p_c, overlap)
        cur = qpos // SLC_BLOCK
        forced = ((blk_ids[None, :] == 0) | (blk_ids[None, :] == cur[:, None])
                  | (blk_ids[None, :] == cur[:, None] - 1))
        future = slc_start[None, :] > qpos[:, None]
        imp = jnp.where(forced, FORCE_SCORE, imp)
        imp = jnp.where(future, -FORCE_SCORE, imp)
        _, idx = lax.top_k(imp, n_top)
        k_sel = ks_blk[bi, gi, idx].reshape(b, G, NSA_QBLOCK, n_top * SLC_BLOCK, d)
        v_sel = vs_blk[bi, gi, idx].reshape(b, G, NSA_QBLOCK, n_top * SLC_BLOCK, d)
        pos_sel = (idx[..., None] * SLC_BLOCK + tok_in_blk).reshape(b, G, NSA_QBLOCK, n_top * SLC_BLOCK)
        bias_sel = table[_rel_bucket(qpos[:, None] - pos_sel), gi].transpose(0, 1, 4, 2, 3)
        s_s = jnp.einsum('bghqd,bgqkd->bghqk', qb, k_sel).astype(jnp.float32) * scale + bias_sel
        p_s = _masked_softmax(s_s, (pos_sel <= qpos[:, None])[:, :, None])
        o_s = jnp.einsum('bghqk,bgqkd->bghqd', p_s.astype(dt), v_sel)
        kwb = lax.dynamic_slice_in_dim(kw_pad, t0, WINDOW + NSA_QBLOCK, axis=2)
        vwb = lax.dynamic_slice_in_dim(vw_pad, t0, WINDOW + NSA_QBLOCK, axis=2)
        pos_w = t0 + win_off
        dist_w = qpos[:, None] - pos_w[None, :]
        m_w = (pos_w[None, :] >= 0) & (dist_w >= 0) & (dist_w < WINDOW)
        s_w = jnp.einsum('bghqd,bgkd->bghqk', qb, kwb).astype(jnp.float32) * scale + bias_2d(dist_w)
        p_w = _masked_softmax(s_w, m_w)
        o_w = jnp.einsum('bghqk,bgkd->bghqd', p_w.astype(dt), vwb)
        return o_c, o_s, o_w

    o_c, o_s, o_w = lax.map(block, jnp.arange(t // NSA_QBLOCK))

    def to_btHd(o):
        return o.transpose(1, 0, 4, 2, 3, 5).reshape(b, t, hb, d)

    g = jax.nn.sigmoid(gate_logits).reshape(b, t, hb, 3)
    o = (g[..., 0:1] * to_btHd(o_c) + g[..., 1:2] * to_btHd(o_s) + g[..., 2:3] * to_btHd(o_w))
    return o.reshape(b, t, hb * d)


def setup_inputs(seed: int = 0) -> dict:
    key = jax.random.key(seed)
    ks = jax.random.split(key, 11)
    f32 = jnp.float32
    x = jax.random.normal(ks[0], (BATCH, SEQ, D_MODEL), f32)
    norm_g = 1.0 + 0.02 * jax.random.normal(ks[1], (DEPTH, D_MODEL), f32)
    w_in = jax.random.normal(ks[2], (DEPTH, D_MODEL, IN_WIDTH), f32) * D_MODEL ** -0.5
    w_out = jax.random.normal(ks[3], (DEPTH, D_MIX, D_MODEL), f32) * D_MIX ** -0.5
    forget_b = 2.0 + 0.5 * jax.random.normal(ks[4], (DEPTH, N_HEADS_FOX), f32)
    cmp_w1 = jax.random.normal(ks[5], (DEPTH, 2, CMP_BLOCK * HEAD_DIM, CMP_HIDDEN), f32) * (CMP_BLOCK * HEAD_DIM) ** -0.5
    cmp_b1 = 0.02 * jax.random.normal(ks[6], (DEPTH, 2, CMP_HIDDEN), f32)
    cmp_w2 = jax.random.normal(ks[7], (DEPTH, 2, CMP_HIDDEN, HEAD_DIM), f32) * CMP_HIDDEN ** -0.5
    cmp_pe = 0.1 * jax.random.normal(ks[8], (DEPTH, 2, CMP_BLOCK, HEAD_DIM), f32)
    rel_bias = 0.5 * jax.random.normal(ks[9], (REL_BUCKETS, N_HEADS_NSA), f32)
    final_g = 1.0 + 0.02 * jax.random.normal(ks[10], (D_MODEL,), f32)
    return {'x': x, 'norm_g': norm_g, 'w_in': w_in, 'w_out': w_out, 'forget_b': forget_b,
            'cmp_w1': cmp_w1, 'cmp_b1': cmp_b1, 'cmp_w2': cmp_w2, 'cmp_pe': cmp_pe,
            'rel_bias': rel_bias, 'final_g': final_g}


def reference(x, norm_g, w_in, w_out, forget_b, cmp_w1, cmp_b1, cmp_w2, cmp_pe, rel_bias, final_g):
    for l in range(DEPTH):
        h = _rmsnorm(x, norm_g[l])
        proj = h @ w_in[l]
        (qa, ka, va, za,
         qb, kc, vc, ksl, vsl, kwn, vwn, gb, zb,
         qc, kcf, vcf, fc, zc) = _split_proj(proj)
        o_a = _merge(_stick_breaking(_heads(qa, N_HEADS_SB), _heads(ka, N_HEADS_SB), _heads(va, N_HEADS_SB)))
        o_a = o_a * jax.nn.silu(za)
        o_b = _nsa(_heads(qb, N_HEADS_NSA), _heads(kc, NSA_KV_GROUPS), _heads(vc, NSA_KV_GROUPS),
                   _heads(ksl, NSA_KV_GROUPS), _heads(vsl, NSA_KV_GROUPS),
                   _heads(kwn, NSA_KV_GROUPS), _heads(vwn, NSA_KV_GROUPS),
                   gb, rel_bias, cmp_w1[l], cmp_b1[l], cmp_w2[l], cmp_pe[l])
        o_b = o_b * jax.nn.silu(zb)
        log_f = jax.nn.log_sigmoid((fc + forget_b[l]).astype(jnp.float32)).transpose(0, 2, 1)
        o_c = _merge(_forgetting_attention(_heads(qc, N_HEADS_FOX), _heads(kcf, N_HEADS_FOX),
                                           _heads(vcf, N_HEADS_FOX), log_f))
        o_c = o_c * jax.nn.silu(zc)
        mix = jnp.concatenate([o_a, o_b, o_c], axis=-1) @ w_out[l]
        x = x + mix
    return _rmsnorm(x, final_g)
```

```python
import contextlib
import numpy as np
import concourse.bass as bass
import concourse.mybir as mybir

F32 = mybir.dt.float32
BF16 = mybir.dt.bfloat16
AF = mybir.ActivationFunctionType
ALU = mybir.AluOpType
AX = mybir.AxisListType

ENGS = ("pe", "act", "dve", "pool", "sp")
_ENG_ATTR = {"pe": "tensor", "act": "scalar", "dve": "vector", "pool": "gpsimd", "sp": "sync"}


class Buf:
    def __init__(self, t, name, psum=False):
        self.t = t
        self.name = name
        self.psum = psum
        self.w = {}
        self.r = {}
        self.pr = {}
        self.w0 = {}

    def __getitem__(self, k):
        return self.t[k]

    def ap(self):
        return self.t.ap() if hasattr(self.t, "ap") and callable(getattr(self.t, "ap")) else self.t[:]


class Ring:
    def __init__(self, bufs):
        self.bufs = bufs
        self.i = 0

    def next(self):
        b = self.bufs[self.i % len(self.bufs)]
        self.i += 1
        return b


class FW:
    def __init__(self, nc, n_dma_sems=12):
        self.nc = nc
        self.es = contextlib.ExitStack()
        self.pes = None
        self.streams = {e: [] for e in ENGS}
        self.phase = 0
        self.cnt = {}
        self.sems = {}
        self.seen = {e: {} for e in ENGS}
        self.dma_pool = {}
        self.n_dma_sems = n_dma_sems
        self.nbuf = 0
        self.barrier = {}
        self.n_ops = 0
        self.n_waits = 0

    def sem(self, key):
        if key not in self.sems:
            self.sems[key] = self.es.enter_context(self.nc.semaphore("s%d" % len(self.sems)))
        return self.sems[key]

    def sb(self, shape, dt, name=None, glob=False):
        self.nbuf += 1
        name = (name or "sb") + "_%d" % self.nbuf
        es = self.es if (glob or self.pes is None) else self.pes
        t = es.enter_context(self.nc.sbuf_tensor(name, list(shape), dt))
        return Buf(t, name)

    def ps(self, shape, dt, name=None, glob=False):
        self.nbuf += 1
        name = (name or "ps") + "_%d" % self.nbuf
        es = self.es if (glob or self.pes is None) else self.pes
        t = es.enter_context(self.nc.psum_tensor(name, list(shape), dt))
        return Buf(t, name, psum=True)

    def phase_begin(self):
        assert self.pes is None
        self.pes = contextlib.ExitStack()
        self.barrier = self.final_events()
        for e in ENGS:
            for k, v in self.barrier.items():
                if self.seen[e].get(k, 0) < v:
                    self.seen[e][k] = v

    def phase_end(self, final=False):
        self.emit(final=final)
        self.streams = {e: [] for e in ENGS}
        self.pes.close()
        self.pes = None

    def dram(self, name, shape, dt, kind="Internal"):
        t = self.nc.dram_tensor(name, list(shape), dt, kind=kind)
        return Buf(t, name)

    def ring(self, n, shape, dt, name, psum=False):
        return Ring([(self.ps if psum else self.sb)(shape, dt, "%s%d" % (name, i)) for i in range(n)])

    def _need(self, eng, deps, out):
        seen = self.seen[eng]
        for k, v in deps.items():
            if eng == "pe" and k[1] == "pe" and k[0] != "dma":
                continue
            if seen.get(k, 0) < v:
                out[k] = max(out.get(k, 0), v)

    def _track(self, eng, ev, reads, writes, pwrites):
        need = {}
        for b in reads:
            self._need(eng, b.w, need)
            if b.psum:
                self._need(eng, {k: v for k, v in b.r.items() if k[1] != eng}, need)
        for b in writes:
            self._need(eng, b.w, need)
            self._need(eng, b.r, need)
        for b in pwrites:
            self._need(eng, b.r, need)
            self._need(eng, b.pr, need)
            self._need(eng, b.w0, need)
        for k, v in need.items():
            self.seen[eng][k] = v
        k, v = ev
        for b in reads:
            b.r[k] = max(b.r.get(k, 0), v)
        for b in writes:
            pr = dict(b.w)
            for kk, vv in b.r.items():
                pr[kk] = max(pr.get(kk, 0), vv)
            b.pr = pr
            b.w = {k: v}
            b.w0 = {k: v}
            b.r = {}
        for b in pwrites:
            b.w[k] = max(b.w.get(k, 0), v)
        self.n_waits += len(need)
        return list(need.items())

    def op(self, eng, fn, reads=(), writes=(), pwrites=()):
        key = (self.phase, eng)
        self.cnt[key] = self.cnt.get(key, 0) + 1
        ev = (key, self.cnt[key])
        waits = self._track(eng, ev, reads, writes, pwrites)
        self.sem(key)
        self.streams[eng].append((waits, fn, key, 1))
        self.n_ops += 1

    def dma(self, out, in_, reads=(), writes=(), pwrites=(), q="sp", **kw):
        pool = self.dma_pool.setdefault(q, {"i": 0, "tot": [0] * self.n_dma_sems})
        i = pool["i"] % self.n_dma_sems
        pool["i"] += 1
        key = ("dma", q, i)
        prev = pool["tot"][i]
        pool["tot"][i] = prev + 16
        ev = (key, prev + 16)
        waits = self._track(q, ev, reads, writes, pwrites)
        if prev > 0 and self.seen[q].get(key, 0) < prev:
            waits.append((key, prev))
            self.seen[q][key] = prev
        self.sem(key)
        self.streams[q].append((waits, lambda e: e.dma_start(out=out, in_=in_, **kw), key, 16))
        self.n_ops += 1

    def mm(self, out, lhsT, rhs, start, stop, reads=(), writes=(), pwrites=()):
        self.op("pe", lambda e: e.matmul(out, lhsT=lhsT, rhs=rhs, start=start, stop=stop), reads, writes, pwrites)

    def tr(self, out, in_, ident, reads=(), writes=(), pwrites=()):
        self.op("pe", lambda e: e.transpose(out=out, in_=in_, identity=ident), reads, writes, pwrites)

    def act(self, out, in_, func, reads=(), writes=(), pwrites=(), **kw):
        self.op("act", lambda e: e.activation(out=out, in_=in_, func=func, **kw), reads, writes, pwrites)

    def ts(self, eng, out, in0, s1, s2, op0, op1=None, reads=(), writes=(), pwrites=()):
        if op1 is None:
            self.op(eng, lambda e: e.tensor_scalar(out=out, in0=in0, scalar1=s1, scalar2=s2, op0=op0), reads, writes, pwrites)
        else:
            self.op(eng, lambda e: e.tensor_scalar(out=out, in0=in0, scalar1=s1, scalar2=s2, op0=op0, op1=op1), reads, writes, pwrites)

    def tt(self, eng, out, in0, in1, op, reads=(), writes=(), pwrites=()):
        self.op(eng, lambda e: e.tensor_tensor(out=out, in0=in0, in1=in1, op=op), reads, writes, pwrites)

    def stt(self, eng, out, in0, scalar, in1, op0, op1, reads=(), writes=(), pwrites=()):
        self.op(eng, lambda e: e.scalar_tensor_tensor(out=out, in0=in0, scalar=scalar, in1=in1, op0=op0, op1=op1), reads, writes, pwrites)

    def cp(self, eng, out, in_, reads=(), writes=(), pwrites=()):
        if eng == "act":
            self.op(eng, lambda e: e.copy(out=out, in_=in_), reads, writes, pwrites)
        else:
            self.op(eng, lambda e: e.tensor_copy(out=out, in_=in_), reads, writes, pwrites)

    def memset(self, eng, ap, val, writes=(), pwrites=()):
        self.op(eng, lambda e: e.memset(ap, val), (), writes, pwrites)

    def final_events(self):
        ev = dict(self.cnt)
        for q, pool in self.dma_pool.items():
            for i, tot in enumerate(pool["tot"]):
                if tot:
                    ev[("dma", q, i)] = tot
        return ev

    def emit(self, final=True):
        nc = self.nc
        fin = self.final_events() if final else {}
        with nc.Block() as block:
            bar = dict(self.barrier)

            def run(e, name):
                for k, v in bar.items():
                    e.wait_ge(self.sems[k], v)
                for waits, fn, key, amt in self.streams[name]:
                    for k, v in waits:
                        e.wait_ge(self.sems[k], v)
                    fn(e).then_inc(self.sems[key], amt)
                if name == "sp":
                    for k, v in fin.items():
                        e.wait_ge(self.sems[k], v)

            @block.tensor
            def _(e):
                run(e, "pe")

            @block.scalar
            def _(e):
                run(e, "act")

            @block.vector
            def _(e):
                run(e, "dve")

            @block.gpsimd
            def _(e):
                run(e, "pool")

            @block.sync
            def _(e):
                run(e, "sp")

    def close(self):
        self.es.close()


import math
import ml_dtypes
from concourse.bass_utils import run_bass_kernel_spmd

T, D, IW, NT, NQC = 4096, 1024, 3868, 32, 8
NB, OFF = 4608, 2064
NEG = -30000.0
NL = 2
NPBF = ml_dtypes.bfloat16
VW = 12 * 65
GW = 1048


class NS:
    pass


def _bucket(n):
    n = np.maximum(n, 0)
    nf = np.maximum(n, 1).astype(np.float32)
    large = 16 + (np.log(nf / np.float32(16)) / np.float32(math.log(128 / 16)) * np.float32(16)).astype(np.int32)
    large = np.minimum(large, 31)
    return np.where(n < 16, n, large)


def _consts():
    c = {}
    eye = np.eye(128, dtype=np.float32)
    c["c_ident"] = eye.astype(NPBF)
    c["c_identf"] = eye.copy()
    c["c_jrev"] = eye[::-1].copy().astype(NPBF)
    c["c_negi"] = (-eye).astype(NPBF)
    j = np.arange(128)[:, None]
    s = np.arange(128)[None, :]
    c["c_tri"] = (j >= s).astype(np.float32).astype(NPBF)
    c["c_ones"] = np.ones((128, 128), np.float32).astype(NPBF)
    sl = np.arange(128)[:, None, None]
    r = np.arange(4)[None, :, None]
    i = np.arange(512)[None, None, :]
    c["c_msb"] = np.where(128 * r + sl >= i, NEG, 0.0).astype(np.float32).astype(NPBF)
    c["c_mfox"] = np.where(128 * r + sl > i, NEG, 0.0).astype(np.float32).astype(NPBF)
    jj = np.arange(64)[:, None, None]
    kt = np.arange(32)[None, :, None]
    k = np.arange(128)[None, None, :]
    c["c_eexp"] = np.where((128 * kt + k) // 64 == jj, 30000.0, 0.0).astype(np.float32).astype(NPBF)
    n = np.arange(NB) - OFF
    b = _bucket(n)
    oh = np.zeros((32, NB), np.float32)
    oh[b[n >= 0], np.nonzero(n >= 0)[0]] = 1.0
    c["c_oh"] = oh
    mrow = np.zeros((8, 2, NB), np.float32)
    mrow[:, 0, n < 0] = NEG
    mrow[:, 1, (n < 0) | (n >= 512)] = NEG
    c["c_mrow"] = mrow
    t = np.arange(T)[:, None]
    jb = np.arange(64)[None, :]
    cur = t // 64
    F = np.zeros((T, 64), np.float32)
    F[(jb == 0) | (jb == cur) | (jb == cur - 1)] = 1e4
    F[np.broadcast_to(jb > cur, F.shape)] = -1e4
    c["c_fst"] = F
    cc = np.arange(256)[:, None]
    cend = cc * 16 + 31
    cstart = cc * 16
    ov = np.clip(np.minimum(cend, jb * 64 + 63) - np.maximum(cstart, jb * 64) + 1, 0, None).astype(np.float32) / 32.0
    ov[255] = 0.0
    c["c_ovl"] = np.ascontiguousarray(ov.reshape(2, 128, 64).transpose(1, 0, 2)).astype(NPBF)
    return c


def _prep_shared(inp):
    m = {}
    f = np.float32
    m["w_in"] = np.ascontiguousarray(inp["w_in"], f)
    m["w_out"] = np.ascontiguousarray(inp["w_out"], f)
    m["g_in"] = np.ascontiguousarray(np.asarray(inp["norm_g"], f).reshape(NL, 8, 128).transpose(2, 0, 1))
    m["fb"] = np.ascontiguousarray(np.asarray(inp["forget_b"], f).T)
    m["w1"] = np.ascontiguousarray(inp["cmp_w1"], f)
    m["b1p"] = np.ascontiguousarray(np.asarray(inp["cmp_b1"], f).reshape(NL, 2, 2, 128).transpose(3, 0, 1, 2))
    m["w2"] = np.ascontiguousarray(inp["cmp_w2"], f)
    m["peT"] = np.ascontiguousarray(np.asarray(inp["cmp_pe"], f).transpose(3, 0, 1, 2))
    m["relb"] = np.ascontiguousarray(inp["rel_bias"], f)
    m["fing"] = np.ascontiguousarray(np.asarray(inp["final_g"], f).reshape(1, D))
    m.update(_consts())
    return m


_IN_SPECS = [
    ("x", [T, D], F32), ("w_in", [NL, D, IW], F32), ("w_out", [NL, D, D], F32), ("g_in", [128, NL, 8], F32),
    ("fb", [4, NL], F32), ("w1", [NL, 2, 2048, 256], F32), ("b1p", [128, NL, 2, 2], F32),
    ("w2", [NL, 2, 256, 64], F32), ("peT", [64, NL, 2, 32], F32), ("relb", [32, 8], F32), ("fing", [1, D], F32),
    ("c_ident", [128, 128], BF16), ("c_identf", [128, 128], F32), ("c_jrev", [128, 128], BF16), ("c_negi", [128, 128], BF16),
    ("c_tri", [128, 128], BF16), ("c_ones", [128, 128], BF16), ("c_msb", [128, 4, 512], BF16),
    ("c_mfox", [128, 4, 512], BF16), ("c_eexp", [64, 32, 128], BF16), ("c_oh", [32, NB], F32),
    ("c_mrow", [8, 2, NB], F32), ("c_fst", [T, 64], F32), ("c_ovl", [128, 2, 64], BF16),
]


def _declare(fw, dbg):
    C = NS()
    C.i = {}
    for name, shape, dt in _IN_SPECS:
        C.i[name] = fw.dram(name, shape, dt, kind="ExternalInput")
    C.out = fw.dram("out", [T, D], F32, kind="ExternalOutput")

    def scr(name, shape, dt):
        return fw.dram(name, shape, dt, kind="ExternalOutput" if name in dbg else "Internal")

    C.xres = scr("xres", [T, D], F32)
    C.sbQ = scr("sbQ", [256, T], BF16)
    C.sbK = scr("sbK", [256, T], BF16)
    C.sbKn = scr("sbKn", [256, T], BF16)
    C.nsaQ = scr("nsaQ", [512, T], BF16)
    C.cmpKin = scr("cmpKin", [128, T], BF16)
    C.cmpVin = scr("cmpVin", [128, T], BF16)
    C.selK = scr("selK", [128, T], BF16)
    C.winK = scr("winK", [128, T], BF16)
    C.foxQ = scr("foxQ", [4, 70, T], BF16)
    C.foxK = scr("foxK", [4, 70, T], BF16)
    C.fcT = scr("fcT", [4, T], F32)
    C.Vtok = scr("Vtok", [NT, 128, VW], BF16)
    C.gates = scr("gates", [T, GW], F32)
    C.bv = {(v, p): scr("bv%d%s" % (v, p), [8, NB], BF16) for v in (0, 1) for p in ("hi", "lo")}
    C.selT = scr("selT", [2, 64, T], BF16)
    C.o_sb = scr("o_sb", [T, 256], F32)
    C.o_cmp = scr("o_cmp", [T, 512], F32)
    C.o_sel = scr("o_sel", [T, 512], F32)
    C.o_win = scr("o_win", [T, 512], F32)
    C.o_fox = scr("o_fox", [T, 256], F32)
    return C


def phase_consts(fw, C):
    fw.phase_begin()

    def gconst(name, shape, dt):
        b = fw.sb(shape, dt, name, glob=True)
        src = C.i[name]
        fw.dma(b[:], src[:], reads=[src], writes=[b])
        return b

    C.ident = gconst("c_ident", [128, 128], BF16)
    C.identf = gconst("c_identf", [128, 128], F32)
    C.jrev = gconst("c_jrev", [128, 128], BF16)
    C.negi = gconst("c_negi", [128, 128], BF16)
    C.tri = gconst("c_tri", [128, 128], BF16)
    C.ones = gconst("c_ones", [128, 128], BF16)
    C.msb = gconst("c_msb", [128, 4, 512], BF16)
    C.mfox = gconst("c_mfox", [128, 4, 512], BF16)
    C.ovl = gconst("c_ovl", [128, 2, 64], BF16)
    C.g_sb = gconst("g_in", [128, NL, 8], F32)
    C.tab31 = fw.sb([128, 8], F32, "tab31", glob=True)
    fw.dma(C.tab31[:], C.i["relb"][31:32, :].broadcast_to([128, 8]), reads=[C.i["relb"]], writes=[C.tab31])
    C.kcmpT = fw.sb([64, 2, 256], BF16, "kcmpT", glob=True)
    C.vcaug = fw.sb([128, 2, 2, 129], BF16, "vcaug", glob=True)

    relb = fw.sb([32, 8], F32, "relb")
    oh = fw.sb([32, NB], F32, "oh")
    mrow = fw.sb([8, 2, NB], F32, "mrow")
    fw.dma(relb[:], C.i["relb"][:], writes=[relb])
    fw.dma(oh[:], C.i["c_oh"][:], writes=[oh])
    fw.dma(mrow[:], C.i["c_mrow"][:], writes=[mrow])
    pr = fw.ring(2, [128, 512], F32, "p0", psum=True)
    bvf = [fw.sb([8, NB], F32, "bvf%d" % v) for v in (0, 1)]
    for ch in range(NB // 512):
        ps = pr.next()
        sl = slice(ch * 512, (ch + 1) * 512)
        fw.mm(ps[0:8, :], relb[:], oh[:, sl], True, True, reads=[relb, oh], writes=[ps])
        for v in (0, 1):
            fw.tt("dve", bvf[v][:, sl], ps[0:8, :], mrow[:, v, sl], ALU.add, reads=[ps, mrow], pwrites=[bvf[v]])
    for v in (0, 1):
        hi = fw.sb([8, NB], BF16, "bvhi%d" % v)
        lo = fw.sb([8, NB], BF16, "bvlo%d" % v)
        fw.cp("dve", hi[:], bvf[v][:], reads=[bvf[v]], writes=[hi])
        fw.tt("dve", lo[:], bvf[v][:], hi[:], ALU.subtract, reads=[bvf[v], hi], writes=[lo])
        fw.dma(C.bv[(v, "hi")][:], hi[:], reads=[hi], pwrites=[C.bv[(v, "hi")]])
        fw.dma(C.bv[(v, "lo")][:], lo[:], reads=[lo], pwrites=[C.bv[(v, "lo")]])
    fw.phase_end()


def _fm_chunks(C):
    ch = []
    for j in range(2):
        ch.append((0 + 128 * j, 128, 0.125, [(0, 128, C.sbQ, 128 * j, None)]))
    for j in range(2):
        ch.append((256 + 128 * j, 128, 1.0, [(0, 128, C.sbK, 128 * j, C.sbKn)]))
    for j in range(4):
        ch.append((1024 + 128 * j, 128, 0.125, [(0, 128, C.nsaQ, 128 * j, None)]))
    ch.append((1536, 128, 1.0, [(0, 128, C.cmpKin, 0, None)]))
    ch.append((1664, 128, 1.0, [(0, 128, C.cmpVin, 0, None)]))
    ch.append((1792, 128, 1.0, [(0, 128, C.selK, 0, None)]))
    ch.append((2048, 128, 1.0, [(0, 128, C.winK, 0, None)]))
    for j in range(2):
        ch.append((2840 + 128 * j, 128, 0.125, [(0, 64, ("fox", C.foxQ, 2 * j), 0, None), (64, 128, ("fox", C.foxQ, 2 * j + 1), 0, None)]))
    for j in range(2):
        ch.append((3096 + 128 * j, 128, 1.0, [(0, 64, ("fox", C.foxK, 2 * j), 0, None), (64, 128, ("fox", C.foxK, 2 * j + 1), 0, None)]))
    ch.append((3608, 4, 1.0, [(0, 4, ("f32", C.fcT), 0, None)]))
    return ch


def phase_proj(fw, C, l, xsrc, stop=None):
    fw.phase_begin()
    hT = fw.sb([128, 8, T], BF16, "hT")
    W = fw.sb([128, 8, IW], BF16, "W")
    wst = fw.ring(3, [128, IW // 4], F32, "wst")
    xr = fw.ring(3, [128, D], F32, "xt")
    junk = fw.sb([128, D], BF16, "junk")
    xnr = fw.ring(2, [128, D], BF16, "xn")
    ssr = fw.ring(2, [128, 1], F32, "ss")
    rsr = fw.ring(2, [128, 1], F32, "rs")
    ptr = fw.ring(2, [128, 8, 128], BF16, "pt", psum=True)
    pacc = fw.ring(5, [128, 512], F32, "pa", psum=True)
    fmo = fw.ring(3, [128, 512], BF16, "fmo")
    fmn = fw.ring(2, [128, 512], BF16, "fmn")
    fmf = fw.ring(2, [4, 512], F32, "fmf")
    vst = fw.ring(2, [128, 12, 65], BF16, "vst")
    gst = fw.ring(2, [128, GW], F32, "gst")
    hw = IW // 4
    win = C.i["w_in"]
    wjobs = [(c, hf) for c in range(8) for hf in range(4)]

    def wload(n):
        for _ in range(n):
            if not wjobs:
                return
            c, hf = wjobs.pop(0)
            st = wst.next()
            fw.dma(st[:], win[l, c * 128:(c + 1) * 128, hf * hw:(hf + 1) * hw], reads=[win], writes=[st])
            fw.ts("pool", W[:, c, hf * hw:(hf + 1) * hw], st[:], C.g_sb[:, l, c:c + 1], None, ALU.mult,
                  reads=[st, C.g_sb], pwrites=[W])

    for b in vst.bufs:
        fw.memset("pool", b[:], 1.0, writes=[b])
    for i in range(NT):
        xt = xr.next()
        fw.dma(xt[:], xsrc[i * 128:(i + 1) * 128, :], reads=[xsrc], writes=[xt])
        wload(1)
        ss = ssr.next()
        rs = rsr.next()
        fw.memset("pool", ss[:], 0.0, writes=[ss])
        fw.act(junk[:], xt[:], AF.Square, accum_out=ss[:], reads=[xt], writes=[junk, ss])
        fw.act(rs[:], ss[:], AF.Sqrt, scale=1.0 / D, bias=1e-6, reads=[ss], writes=[rs])
        fw.op("dve", lambda e, o=rs[:]: e.reciprocal(out=o, in_=o), reads=[rs], writes=[rs])
        xn = xnr.next()
        fw.ts("dve", xn[:], xt[:], rs[:, 0:1], None, ALU.mult, reads=[xt, rs], writes=[xn])
        pt = ptr.next()
        for c in range(8):
            fw.tr(pt[:, c, :], xn[:, c * 128:(c + 1) * 128], C.ident[:], reads=[xn, C.ident],
                  writes=[pt] if c == 0 else (), pwrites=() if c == 0 else [pt])
        fw.cp("dve", hT[:, :, i * 128:(i + 1) * 128], pt[:], reads=[pt], pwrites=[hT])
    wload(100)
    if stop == 2:
        fw.phase_end()
        return
    ev = 0
    import os
    chunks = _fm_chunks(C)
    if os.environ.get("K_NOFC"):
        chunks = chunks[:-1]
    if os.environ.get("K_FEW"):
        chunks = chunks[:int(os.environ["K_FEW"])]
    for (c0, M, scale, dests) in chunks:
        for qc in range(NQC):
            ps = pacc.next()
            tsl = slice(qc * 512, (qc + 1) * 512)
            for k in range(8):
                fw.mm(ps[0:M, :], W[:, k, c0:c0 + M], hT[:, k, tsl], k == 0, k == 7, reads=[W, hT],
                      writes=[ps] if k == 0 else (), pwrites=() if k == 0 else [ps])
            if isinstance(dests[0][2], tuple) and dests[0][2][0] == "f32":
                o = fmf.next()
                fw.cp("dve", o[:], ps[0:M, :], reads=[ps], writes=[o])
                dd = dests[0][2][1]
                fw.dma(dd[:, tsl], o[:], reads=[o], pwrites=[dd])
                continue
            o = fmo.next()
            eng = "dve" if (ev % 2 == 0 or os.environ.get("K_NOACT")) else "act"
            ev += 1
            if eng == "act":
                fw.op("act", lambda e, oo=o[0:M, :], ii=ps[0:M, :], sc=scale: e.mul(out=oo, in_=ii, mul=sc), reads=[ps], writes=[o])
            else:
                fw.ts("dve", o[0:M, :], ps[0:M, :], scale, None, ALU.mult, reads=[ps], writes=[o])
            for (r0, r1, dd, drow, nd) in dests:
                if isinstance(dd, tuple):
                    _, dbuf, hh = dd
                    fw.dma(dbuf[hh, 0:64, tsl], o[r0:r1, :], reads=[o], pwrites=[dbuf])
                else:
                    fw.dma(dd[drow + r0:drow + r1, tsl], o[r0:r1, :], reads=[o], pwrites=[dd])
                if nd is not None:
                    o2 = fmn.next()
                    fw.ts("dve", o2[:], ps[0:M, :], -1.0, None, ALU.mult, reads=[ps], writes=[o2])
                    fw.dma(nd[drow + r0:drow + r1, tsl], o2[r0:r1, :], reads=[o2], pwrites=[nd])
    if stop == 3:
        fw.phase_end()
        return
    groups = [
        (512, 512, [(0, 256, "v", 0), (256, 512, "g", 0)]),
        (1920, 128, [(0, 128, "v", 4)]),
        (2176, 512, [(0, 128, "v", 6), (128, 512, "g", 256)]),
        (2688, 152, [(0, 152, "g", 640)]),
        (3352, 256, [(0, 256, "v", 8)]),
        (3612, 256, [(0, 256, "g", 792)]),
    ]
    for i in range(NT):
        vs = vst.next()
        gs = gst.next()
        first_v = True
        first_g = True
        for (c0, N, outs) in groups:
            ps = pacc.next()
            for k in range(8):
                fw.mm(ps[:, 0:N], hT[:, k, i * 128:(i + 1) * 128], W[:, k, c0:c0 + N], k == 0, k == 7, reads=[W, hT],
                      writes=[ps] if k == 0 else (), pwrites=() if k == 0 else [ps])
            for (a, b, kind, dcol) in outs:
                eng = "dve" if ev % 2 == 0 else "act"
                ev += 1
                if kind == "v":
                    nh = (b - a) // 64
                    src = ps[:, a:b].rearrange("p (h d) -> p h d", d=64)
                    dst = vs[:, dcol:dcol + nh, 0:64]
                    wr = dict(writes=[vs]) if first_v else dict(pwrites=[vs])
                    first_v = False
                else:
                    src = ps[:, a:b]
                    dst = gs[:, dcol:dcol + (b - a)]
                    wr = dict(writes=[gs]) if first_g else dict(pwrites=[gs])
                    first_g = False
                fw.cp(eng, dst, src, reads=[ps], **wr)
        fw.dma(C.Vtok[i, :, :], vs[:].rearrange("p h d -> p (h d)"), reads=[vs], pwrites=[C.Vtok])
        fw.dma(C.gates[i * 128:(i + 1) * 128, :], gs[:], reads=[gs], pwrites=[C.gates])
    fw.phase_end()


def phase_foxaux(fw, C, l):
    fw.phase_begin()
    fc = fw.sb([4, T], F32, "fc")
    fbt = fw.sb([4, NL], F32, "fbt")
    nb = fw.sb([4, 1], F32, "nb")
    fw.dma(fc[:], C.fcT[:], reads=[C.fcT], writes=[fc])
    fw.dma(fbt[:], C.i["fb"][:], writes=[fbt])
    fw.ts("dve", nb[:], fbt[:, l:l + 1], -1.0, None, ALU.mult, reads=[fbt], writes=[nb])
    ones = fw.sb([4, T], F32, "ones4")
    cs = fw.sb([4, T], F32, "cs")
    r1 = fw.sb([4, T], F32, "r1")
    AQ = fw.sb([4, 6, T], BF16, "augq")
    AK = fw.sb([4, 6, T], BF16, "augk")
    fw.act(fc[:], fc[:], AF.Exp, scale=-1.0, bias=nb[:, 0:1], reads=[nb], writes=[fc])
    fw.act(fc[:], fc[:], AF.Ln, bias=1.0, writes=[fc])
    fw.memset("pool", ones[:], 1.0, writes=[ones])
    fw.op("dve", lambda e: e.tensor_tensor_scan(out=cs[:], data0=ones[:], data1=fc[:], initial=0.0,
                                                op0=ALU.mult, op1=ALU.add), reads=[ones, fc], writes=[cs])
    fw.memset("pool", AQ[:, 3:6, :], 1.0, writes=[AQ])
    fw.memset("pool", AK[:, 0:3, :], 1.0, writes=[AK])
    fw.cp("dve", AK[:, 3, :], cs[:], reads=[cs], pwrites=[AK])
    fw.tt("dve", r1[:], cs[:], AK[:, 3, :], ALU.subtract, reads=[cs, AK], writes=[r1])
    fw.cp("dve", AK[:, 4, :], r1[:], reads=[r1], pwrites=[AK])
    fw.tt("dve", r1[:], r1[:], AK[:, 4, :], ALU.subtract, reads=[AK], writes=[r1])
    fw.cp("dve", AK[:, 5, :], r1[:], reads=[r1], pwrites=[AK])
    fw.ts("dve", AQ[:, 0:3, :], AK[:, 3:6, :], -1.0, None, ALU.mult, reads=[AK], pwrites=[AQ])
    fw.dma(C.foxQ[:, 64:70, :], AQ[:], reads=[AQ], pwrites=[C.foxQ])
    fw.dma(C.foxK[:, 64:70, :], AK[:], reads=[AK], pwrites=[C.foxK])
    fw.phase_end()


class AttnRes:
    def __init__(self, fw, nv, tmode=False):
        self.nv = nv
        self.tmode = tmode
        if tmode:
            self.OT = fw.ring(2, [128, 512], F32, "OT", psum=True)
            self.otc = fw.ring(2, [128, 512], F32, "otc")
        self.spb = 4 if nv * 4 * 4 <= 2048 else 2
        self.nob = 4 // self.spb
        self.S = fw.ring(3, [128, 512], F32, "S", psum=True)
        self.O = [[fw.ps([128, self.spb, nv], F32, "O%d_%d" % (a, b)) for b in range(self.nob)] for a in range(2)]
        self.oi = 0
        self.P = fw.ring(3, [128, 512], BF16, "P")

    def next_O(self):
        o = self.O[self.oi % 2]
        self.oi += 1
        return o


def attn_jobs(fw, R, jobs, C=None):
    state = {"O": None}
    C_identf_buf = C.identf if C is not None else None
    C_identf = C.identf if C is not None else None

    def emit_S(j):
        S = R.S.next()
        n = 1 + len(j["extras"])
        qb, qa = j["q"]
        kb, ka = j["k"]
        fw.mm(S[:], ka, qa, True, n == 1, reads=[kb, qb], writes=[S])
        for idx, (lb, la, rb, ra) in enumerate(j["extras"]):
            fw.mm(S[:], la, ra, False, idx == n - 2, reads=[lb, rb], pwrites=[S])
        P = R.P.next()
        if j["bias"] is None:
            fw.act(P[:], S[:], AF.Exp, reads=[S], writes=[P])
        else:
            bb, ba = j["bias"]
            fw.act(P[:], S[:], AF.Exp, bias=ba, reads=[S, bb], writes=[P])
        return P

    pending = []

    def emit_PV(j, P):
        vb, va = j["v"]
        if R.tmode:
            if j["first"]:
                state["OT"] = R.OT.next()
            OT = state["OT"]
            fw.mm(OT[0:R.nv, :], va, P[:], j["first"], j["last"], reads=[vb, P],
                  writes=[OT] if j["first"] else (), pwrites=() if j["first"] else [OT])
            if j["last"]:
                otc = R.otc.next()
                fw.cp("act", otc[0:R.nv, :], OT[0:R.nv, :], reads=[OT], writes=[otc])

                def fin(j=j, otc=otc):
                    O = R.next_O()
                    for sl in range(4):
                        fw.tr(O[0][:, sl, :], otc[0:R.nv, sl * 128:(sl + 1) * 128], C_identf[0:R.nv, 0:R.nv],
                              reads=[otc, C_identf_buf], writes=[O[0]] if sl == 0 else (), pwrites=() if sl == 0 else [O[0]])
                    j["done"](O)
                pending.append(fin)
            return
        if j["first"]:
            state["O"] = R.next_O()
        O = state["O"]
        for sl in range(4):
            ob = O[sl // R.spb]
            first_in_bank = j["first"] and (sl % R.spb == 0)
            fw.op("pe", lambda e, o=ob[:, sl % R.spb, :], l=P[:, sl * 128:(sl + 1) * 128], r=va, st=first_in_bank:
                  e.matmul(o, lhsT=l, rhs=r, start=st, stop=False, skip_group_check=True),
                  reads=[P, vb], writes=[ob] if first_in_bank else (), pwrites=() if first_in_bank else [ob])
        if j["last"]:
            j["done"](O)

    prev = None
    for j in jobs:
        P = emit_S(j)
        while pending:
            pending.pop(0)()
        if prev is not None:
            emit_PV(*prev)
        prev = (j, P)
    if prev is not None:
        emit_PV(*prev)
    while pending:
        pending.pop(0)()


def evac_norm(fw, C, O, spb, nv_out, dst, qc, col0, tmp_ring, rz_ring, extra=None):
    ot = tmp_ring.next()
    rz = rz_ring.next()
    for sl in range(4):
        ob = O[sl // spb]
        oo = ob[:, sl % spb, :]
        fw.ts("dve", rz[:, sl:sl + 1], oo[:, 64:65], 1e-30, None, ALU.max, reads=[ob], writes=[rz] if sl == 0 else (),
              pwrites=() if sl == 0 else [rz])
    fw.op("dve", lambda e, o=rz[:]: e.reciprocal(out=o, in_=o), reads=[rz], writes=[rz])
    for sl in range(4):
        ob = O[sl // spb]
        oo = ob[:, sl % spb, :]
        fw.ts("dve", ot[:, sl, :], oo[:, 0:64], rz[:, sl:sl + 1], None, ALU.mult, reads=[ob, rz],
              writes=[ot] if sl == 0 else (), pwrites=() if sl == 0 else [ot])
        if extra is not None:
            extra(sl, ob, oo, rz)
    fw.dma(dst[qc * 512:(qc + 1) * 512, col0:col0 + 64].rearrange("(j p) d -> p j d", p=128), ot[:], reads=[ot], pwrites=[dst])


def load_v(fw, C, ring, h0, nh):
    v = ring.next()
    fw.dma(v[:], C.Vtok[:, :, h0 * 65:(h0 + nh) * 65].rearrange("i p w -> p i w"), reads=[C.Vtok], writes=[v])
    return v


def phase_fox(fw, C, l):
    fw.phase_begin()
    R = AttnRes(fw, 65, tmode=True)
    qr = fw.ring(2, [70, T], BF16, "fq")
    kr = fw.ring(2, [70, T], BF16, "fk")
    vr = fw.ring(2, [128, NT, 65], BF16, "fv")
    otr = fw.ring(2, [128, 4, 64], F32, "fot")
    rzr = fw.ring(2, [128, 4], F32, "frz")
    loaded = {}

    def load(h):
        if h >= 4 or h in loaded:
            return
        q = qr.next()
        k = kr.next()
        fw.dma(q[:], C.foxQ[h, :, :], reads=[C.foxQ], writes=[q])
        fw.dma(k[:], C.foxK[h, :, :], reads=[C.foxK], writes=[k])
        v = load_v(fw, C, vr, 8 + h, 1)
        loaded[h] = (q, k, v)

    def jobs():
        for h in range(4):
            load(h)
            q, k, v = loaded[h]
            for qc in range(NQC):
                nk = 4 * qc + 4
                for kt in range(nk):
                    ex = []
                    if kt >= 4 * qc:
                        ex.append((C.ident, C.ident[:], C.mfox, C.mfox[:, kt - 4 * qc, :]))
                    yield dict(q=(q, q[:, qc * 512:(qc + 1) * 512]), k=(k, k[:, kt * 128:(kt + 1) * 128]), extras=ex, bias=None,
                               v=(v, v[:, kt, :]), first=kt == 0, last=kt == nk - 1,
                               done=lambda O, qc=qc, h=h: evac_norm(fw, C, O, R.spb, 64, C.o_fox, qc, h * 64, otr, rzr))
                    load(h + 1)

    attn_jobs(fw, R, jobs(), C)
    fw.phase_end()


def phase_sb(fw, C, l):
    fw.phase_begin()
    Zr = fw.ring(2, [128, 512], F32, "Z", psum=True)
    Rr = fw.ring(2, [128, 512], F32, "Rk", psum=True)
    Or = fw.ring(2, [128, 4, 64], F32, "Osb", psum=True)
    OTs = [fw.ps([128, 512], F32, "OTsb%d" % i) for i in range(2)]
    otcr = fw.ring(2, [128, 512], F32, "otcsb")
    e1r = fw.ring(2, [128, 512], F32, "e1")
    spr = fw.ring(4, [128, 512], BF16, "spb")
    wr = fw.ring(4, [128, 512], BF16, "wt")
    accs = [fw.sb([128, 512], F32, "acc%d" % i) for i in range(2)]
    abr = [fw.ring(2, [128, 512], BF16, "accb%d" % i) for i in range(2)]
    otr = fw.ring(2, [128, 4, 64], F32, "sot")
    hd = []
    for h in range(4):
        q = fw.sb([64, T], BF16, "sq%d" % h)
        k = fw.sb([64, T], BF16, "sk%d" % h)
        kn = fw.sb([64, T], BF16, "skn%d" % h)
        v = fw.sb([128, NT, 65], BF16, "sv%d" % h)
        hd.append((q, k, kn, v))
    for h in (0, 2, 1, 3):
        q, k, kn, v = hd[h]
        fw.dma(q[:], C.sbQ[h * 64:(h + 1) * 64, :], reads=[C.sbQ], writes=[q])
        fw.dma(k[:], C.sbK[h * 64:(h + 1) * 64, :], reads=[C.sbK], writes=[k])
        fw.dma(kn[:], C.sbKn[h * 64:(h + 1) * 64, :], reads=[C.sbKn], writes=[kn])
        fw.dma(v[:], C.Vtok[:, :, h * 65:(h + 1) * 65].rearrange("i p w -> p i w"), reads=[C.Vtok], writes=[v])

    def tiles(heads):
        for h in heads:
            for qc in range(NQC):
                kts = list(range(4 * qc + 3, -1, -1))
                for idx, kt in enumerate(kts):
                    yield (h, qc, idx, kt, idx == len(kts) - 1)

    def stage_a(tl):
        h, qc, idx, kt, last = tl
        q, k, kn, v = hd[h]
        r = kt - 4 * qc
        Z = Zr.next()
        qa = q[:, qc * 512:(qc + 1) * 512]
        fw.mm(Z[:], k[:, kt * 128:(kt + 1) * 128], qa, True, r < 0, reads=[k, q], writes=[Z])
        if r >= 0:
            fw.mm(Z[:], C.ident[:], C.msb[:, r, :], False, True, reads=[C.ident, C.msb], pwrites=[Z])
        e1 = e1r.next()
        fw.act(e1[:], Z[:], AF.Exp, reads=[Z], writes=[e1])
        sp = spr.next()
        fw.act(sp[:], e1[:], AF.Ln, bias=1.0, reads=[e1], writes=[sp])
        return sp

    def stage_b(si, st, tl, sp):
        h, qc, idx, kt, last = tl
        q, k, kn, v = hd[h]
        r = kt - 4 * qc
        qa = q[:, qc * 512:(qc + 1) * 512]
        Rk = Rr.next()
        fw.mm(Rk[:], C.tri[:], sp[:], True, False, reads=[C.tri, sp], writes=[Rk])
        if idx > 0:
            fw.mm(Rk[:], C.ones[:], st["accb"][:], False, False, reads=[C.ones, st["accb"]], pwrites=[Rk])
        fw.mm(Rk[:], kn[:, kt * 128:(kt + 1) * 128], qa, False, r < 0, reads=[kn, q], pwrites=[Rk])
        if r >= 0:
            fw.mm(Rk[:], C.negi[:], C.msb[:, r, :], False, True, reads=[C.negi, C.msb], pwrites=[Rk])
        w = wr.next()
        fw.act(w[:], Rk[:], AF.Exp, scale=-1.0, reads=[Rk], writes=[w])
        OT = OTs[si]
        fw.mm(OT[0:64, :], v[:, kt, 0:64], w[:], idx == 0, last, reads=[v, w],
              writes=[OT] if idx == 0 else (), pwrites=() if idx == 0 else [OT])
        if last:
            otc = otcr.next()
            fw.cp("dve", otc[0:64, :], OT[0:64, :], reads=[OT], writes=[otc])
            O = Or.next()
            for sl in range(4):
                fw.tr(O[:, sl, :], otc[0:64, sl * 128:(sl + 1) * 128], C.identf[0:64, 0:64], reads=[otc, C.identf],
                      writes=[O] if sl == 0 else (), pwrites=() if sl == 0 else [O])
            ot = otr.next()
            fw.cp("dve", ot[:], O[:], reads=[O], writes=[ot])
            fw.dma(C.o_sb[qc * 512:(qc + 1) * 512, h * 64:(h + 1) * 64].rearrange("(j p) d -> p j d", p=128), ot[:],
                   reads=[ot], pwrites=[C.o_sb])
        else:
            acc = accs[si]
            if idx == 0:
                fw.cp("pool", acc[:], sp[:], reads=[sp], writes=[acc])
            else:
                fw.tt("pool", acc[:], acc[:], sp[:], ALU.add, reads=[sp], writes=[acc])
            ab = abr[si].next()
            fw.cp("dve", ab[:], acc[:], reads=[acc], writes=[ab])
            st["accb"] = ab

    streams = [dict(gen=tiles([0, 1]), prev=None, accb=None), dict(gen=tiles([2, 3]), prev=None, accb=None)]
    alive = True
    while alive:
        alive = False
        for si, st in enumerate(streams):
            tl = next(st["gen"], None)
            sp = stage_a(tl) if tl is not None else None
            if st["prev"] is not None:
                stage_b(si, st, *st["prev"])
            st["prev"] = (tl, sp) if tl is not None else None
            if st["prev"] is not None:
                alive = True
    fw.phase_end()


def phase_compress(fw, C, l):
    fw.phase_begin()
    w1 = fw.sb([64, 32, 256], BF16, "w1")
    w1s = fw.ring(2, [64, 8, 256], F32, "w1s")
    w2 = fw.sb([128, 2, 64], BF16, "w2")
    w2s = fw.sb([128, 2, 64], F32, "w2s")
    pes = fw.sb([64, 32], F32, "pes")
    peb = fw.sb([64, 32], BF16, "peb")
    b1 = fw.sb([128, 2], F32, "b1")
    cb = fw.sb([128, 2], F32, "cb")
    xin = fw.ring(2, [64, T], BF16, "xin")
    hs = fw.ring(2, [128, 2, 256], BF16, "hs")
    pr = fw.ring(3, [128, 512], F32, "pc", psum=True)
    fw.memset("pool", C.vcaug[:], 1.0, writes=[C.vcaug])
    fw.memset("pool", C.kcmpT[:], 0.0, writes=[C.kcmpT])
    for g in range(2):
        fw.cp("pool", C.vcaug[:, g, :, 65:129], C.ovl[:], reads=[C.ovl], pwrites=[C.vcaug])
    for kv in range(2):
        src = C.i["w1"][l, kv].rearrange("(tok d) h -> d tok h", d=64)
        for pc in range(4):
            st = w1s.next()
            fw.dma(st[:], src[:, pc * 8:(pc + 1) * 8, :], writes=[st])
            fw.cp("dve" if pc % 2 == 0 else "pool", w1[:, pc * 8:(pc + 1) * 8, :], st[:], reads=[st],
                  writes=[w1] if pc == 0 else (), pwrites=() if pc == 0 else [w1])
        fw.dma(w2s[:], C.i["w2"][l, kv].rearrange("(hf p) d -> p hf d", p=128), writes=[w2s])
        fw.cp("dve", w2[:], w2s[:], reads=[w2s], writes=[w2])
        fw.dma(pes[:], C.i["peT"][:, l, kv, :], writes=[pes])
        fw.cp("dve", peb[:], pes[:], reads=[pes], writes=[peb])
        fw.dma(b1[:], C.i["b1p"][:, l, kv, :], writes=[b1])
        for hf in range(2):
            ps = pr.next()
            for tok in range(32):
                fw.mm(ps[:, 0:1], w1[:, tok, hf * 128:(hf + 1) * 128], peb[:, tok:tok + 1], tok == 0, tok == 31,
                      reads=[w1, peb], writes=[ps] if tok == 0 else (), pwrites=() if tok == 0 else [ps])
            fw.tt("dve", cb[:, hf:hf + 1], ps[:, 0:1], b1[:, hf:hf + 1], ALU.add, reads=[ps, b1],
                  writes=[cb] if hf == 0 else (), pwrites=() if hf == 0 else [cb])
        srcT = C.cmpKin if kv == 0 else C.cmpVin
        for g in range(2):
            x = xin.next()
            fw.dma(x[:], srcT[g * 64:(g + 1) * 64, :], reads=[srcT], writes=[x])
            xv = x[:].rearrange("d (j t) -> d j t", t=16)
            h_ = hs.next()
            fw.memset("pool", h_[:], 0.0, writes=[h_])
            for hf in range(2):
                ps = pr.next()
                for tok in range(32):
                    rhs = xv[:, 0:255, tok] if tok < 16 else xv[:, 1:256, tok - 16]
                    fw.mm(ps[:, 0:255], w1[:, tok, hf * 128:(hf + 1) * 128], rhs, tok == 0, tok == 31,
                          reads=[w1, x], writes=[ps] if tok == 0 else (), pwrites=() if tok == 0 else [ps])
                fw.act(h_[:, hf, 0:255], ps[:, 0:255], AF.Silu, bias=cb[:, hf:hf + 1], reads=[ps, cb], pwrites=[h_])
            if kv == 0:
                ps = pr.next()
                for hf in range(2):
                    fw.mm(ps[0:64, 0:255], w2[:, hf, :], h_[:, hf, 0:255], hf == 0, hf == 1, reads=[w2, h_],
                          writes=[ps] if hf == 0 else (), pwrites=() if hf == 0 else [ps])
                fw.cp("dve", C.kcmpT[:, g, 0:255], ps[0:64, 0:255], reads=[ps], pwrites=[C.kcmpT])
            else:
                for ct in range(2):
                    ps = pr.next()
                    for hf in range(2):
                        fw.mm(ps[:, 0:64], h_[:, hf, ct * 128:(ct + 1) * 128], w2[:, hf, :], hf == 0, hf == 1, reads=[w2, h_],
                              writes=[ps] if hf == 0 else (), pwrites=() if hf == 0 else [ps])
                    fw.cp("dve", C.vcaug[:, g, ct, 0:64], ps[:, 0:64], reads=[ps], pwrites=[C.vcaug])
    fw.phase_end()


def hankel_load(fw, C, buf, idx, v, part, head, idx0, pstep):
    src = C.bv[(v, part)]
    ap = bass.AP(src.t, head * NB + idx0, [[pstep, 128], [1, 512]])
    fw.dma(buf[:, idx, :], ap, reads=[src], pwrites=[buf])


def phase_cmp(fw, C, l):
    fw.phase_begin()
    R = AttnRes(fw, 129)
    ptp = fw.ps([64, 512], BF16, "ptp")
    qr = fw.ring(2, [64, T], BF16, "cq")
    hkr = fw.ring(2, [128, 18, 512], BF16, "chk")
    otr = fw.ring(2, [128, 4, 64], F32, "cot")
    rzr = fw.ring(2, [128, 4], F32, "crz")
    imp = fw.sb([128, NT, 64], F32, "imp")
    fst = fw.sb([128, NT, 64], F32, "fst")
    vv = fw.ring(2, [128, 64], F32, "vv")
    v2 = fw.ring(2, [128, 64], F32, "v2")
    mx = fw.ring(2, [128, 16], F32, "mx")
    selr = fw.ring(2, [128, 64], BF16, "sel")
    selTs = fw.sb([64, T], BF16, "selTs")
    fw.dma(fst[:], C.i["c_fst"].ap().rearrange("(tt p) j -> p tt j", p=128) if False else
           C.i["c_fst"][:, :].rearrange("(tt p) j -> p tt j", p=128), writes=[fst])
    tiles_of = {}
    for qc in range(NQC):
        tl = []
        if qc <= 4:
            tl.append((0, True))
        else:
            tl.append((0, False))
        if qc >= 4:
            tl.append((1, True))
        tiles_of[qc] = tl
    hk_index = {}
    n = 0
    for qc in range(NQC):
        for (kt, hank) in tiles_of[qc]:
            if hank:
                hk_index[(qc, kt)] = n
                n += 1
    assert n == 9
    loaded = {}

    def load(gh):
        if gh >= 8 or gh in loaded:
            return
        q = qr.next()
        hk = hkr.next()
        fw.dma(q[:], C.nsaQ[gh * 64:(gh + 1) * 64, :], reads=[C.nsaQ], writes=[q])
        first = True
        for (qc, kt), ix in hk_index.items():
            idx0 = OFF + 512 * qc - 2048 * kt - 2063
            for pi, part in enumerate(("hi", "lo")):
                src = C.bv[(0, part)]
                ap = bass.AP(src.t, gh * NB + idx0, [[16, 128], [1, 512]])
                fw.dma(hk[:, 2 * ix + pi, :], ap, reads=[src], writes=[hk] if first else (), pwrites=() if first else [hk])
                first = False
        loaded[gh] = (q, hk)

    def evac(O, g, h, qc):
        gh = 4 * g + h

        def extra(sl, ob, oo, rz):
            tt_ = qc * 4 + sl
            if h == 0:
                fw.ts("dve", imp[:, tt_, :], oo[:, 65:129], rz[:, sl:sl + 1], None, ALU.mult, reads=[ob, rz], pwrites=[imp])
            else:
                fw.stt("dve", imp[:, tt_, :], oo[:, 65:129], rz[:, sl:sl + 1], imp[:, tt_, :], ALU.mult, ALU.add,
                       reads=[ob, rz], pwrites=[imp])
        evac_norm(fw, C, O, R.spb, 64, C.o_cmp, qc, gh * 64, otr, rzr, extra=extra)

    def jobs(g):
        for h in range(4):
            gh = 4 * g + h
            load(gh)
            q, hk = loaded[gh]
            for qc in range(NQC):
                tl = tiles_of[qc]
                for ti, (kt, hank) in enumerate(tl):
                    ex = []
                    bias = None
                    if hank:
                        ix = hk_index[(qc, kt)]
                        ex.append((C.jrev, C.jrev[:], hk, hk[:, 2 * ix, :]))
                        ex.append((C.jrev, C.jrev[:], hk, hk[:, 2 * ix + 1, :]))
                    else:
                        bias = (C.tab31, C.tab31[:, gh:gh + 1])
                    yield dict(q=(q, q[:, qc * 512:(qc + 1) * 512]), k=(C.kcmpT, C.kcmpT[:, g, kt * 128:(kt + 1) * 128]),
                               extras=ex, bias=bias, v=(C.vcaug, C.vcaug[:, g, kt, :]), first=ti == 0, last=ti == len(tl) - 1,
                               done=lambda O, g=g, h=h, qc=qc: evac(O, g, h, qc))
                    load(gh + 1)

    for g in range(2):
        attn_jobs(fw, R, jobs(g))
        for tt_ in range(NT):
            v_ = vv.next()
            fw.tt("dve", v_[:], imp[:, tt_, :], fst[:, tt_, :], ALU.add, reads=[imp, fst], writes=[v_])
            m_ = mx.next()
            w_ = v2.next()
            fw.op("dve", lambda e, o=m_[:, 0:8], i_=v_[:]: e.max(out=o, in_=i_), reads=[v_], writes=[m_])
            fw.op("dve", lambda e, o=w_[:], a=m_[:, 0:8], b=v_[:]: e.match_replace(out=o, in_to_replace=a, in_values=b, imm_value=-1e30),
                  reads=[v_, m_], writes=[w_])
            fw.op("dve", lambda e, o=m_[:, 8:16], i_=w_[:]: e.max(out=o, in_=i_), reads=[w_], pwrites=[m_])
            sl_ = selr.next()
            fw.ts("dve", sl_[:], v_[:], m_[:, 15:16], 1.0, ALU.is_ge, ALU.subtract, reads=[v_, m_], writes=[sl_])
            j4 = tt_ % 4
            fw.tr(ptp[:, j4 * 128:(j4 + 1) * 128], sl_[:], C.ident[:], reads=[sl_, C.ident],
                  writes=[ptp] if j4 == 0 else (), pwrites=() if j4 == 0 else [ptp])
            if j4 == 3:
                fw.cp("dve", selTs[:, (tt_ - 3) * 128:(tt_ + 1) * 128], ptp[:], reads=[ptp], pwrites=[selTs])
        fw.dma(C.selT[g], selTs[:], reads=[selTs], pwrites=[C.selT])
    fw.phase_end()


def phase_selwin(fw, C, l):
    fw.phase_begin()
    R = AttnRes(fw, 65, tmode=True)
    qr = fw.ring(2, [128, T], BF16, "nq")
    hkr = fw.ring(2, [128, 26, 512], BF16, "nhk")
    gk = fw.ring(2, [128, 2, T], BF16, "gk")
    gv = fw.ring(2, [128, NT, 2, 65], BF16, "gvv")
    otr = fw.ring(2, [128, 4, 64], F32, "not")
    rzr = fw.ring(2, [128, 4], F32, "nrz")
    SELD = [-384, -256, -128, 0, 128]
    WIND = [-384, -256, -128, 0, 128, 256, 384, 512]
    loaded = {}
    gl = {}
    eflat = C.i["c_eexp"][:, :, :].rearrange("j a b -> j (a b)")

    def load_group(g):
        if g >= 2 or g in gl:
            return
        k = gk.next()
        v = gv.next()
        fw.dma(k[0:64, 0, :], C.selK[g * 64:(g + 1) * 64, :], reads=[C.selK], writes=[k])
        fw.dma(k[64:128, 0, :], eflat, pwrites=[k])
        fw.dma(k[0:64, 1, :], C.winK[g * 64:(g + 1) * 64, :], reads=[C.winK], pwrites=[k])
        fw.dma(v[:, :, 0, :], C.Vtok[:, :, (4 + g) * 65:(5 + g) * 65].rearrange("i p w -> p i w"), reads=[C.Vtok], writes=[v])
        fw.dma(v[:, :, 1, :], C.Vtok[:, :, (6 + g) * 65:(7 + g) * 65].rearrange("i p w -> p i w"), reads=[C.Vtok], pwrites=[v])
        gl[g] = (k, v)

    def load(gh):
        if gh >= 8 or gh in loaded:
            return
        q = qr.next()
        hk = hkr.next()
        fw.dma(q[0:64, :], C.nsaQ[gh * 64:(gh + 1) * 64, :], reads=[C.nsaQ], writes=[q])
        fw.dma(q[64:128, :], C.selT[gh // 4], reads=[C.selT], pwrites=[q])
        first = True
        for vi, Ds in ((0, SELD), (1, WIND)):
            for di, Dv in enumerate(Ds):
                ix = (0 if vi == 0 else 5) + di
                for pi, part in enumerate(("hi", "lo")):
                    src = C.bv[(vi, part)]
                    ap = bass.AP(src.t, gh * NB + OFF + Dv - 127, [[1, 128], [1, 512]])
                    fw.dma(hk[:, 2 * ix + pi, :], ap, reads=[src], writes=[hk] if first else (), pwrites=() if first else [hk])
                    first = False
        loaded[gh] = (q, hk)

    def jobs():
        for gh in range(8):
            g = gh // 4
            load_group(g)
            load(gh)
            q, hk = loaded[gh]
            k, v = gl[g]
            for br in range(2):
                for qc in range(NQC):
                    kts = list(range(4 * qc + 4)) if br == 0 else list(range(max(0, 4 * qc - 4), 4 * qc + 4))
                    for ti, kt in enumerate(kts):
                        Dv = 512 * qc - 128 * kt
                        ex = []
                        bias = None
                        if br == 0:
                            qa = q[:, qc * 512:(qc + 1) * 512]
                            ka = k[:, 0, kt * 128:(kt + 1) * 128]
                            if Dv <= 128:
                                ix = SELD.index(Dv)
                                ex.append((C.jrev, C.jrev[:], hk, hk[:, 2 * ix, :]))
                                ex.append((C.jrev, C.jrev[:], hk, hk[:, 2 * ix + 1, :]))
                            else:
                                bias = (C.tab31, C.tab31[:, gh:gh + 1])
                        else:
                            qa = q[0:64, qc * 512:(qc + 1) * 512]
                            ka = k[0:64, 1, kt * 128:(kt + 1) * 128]
                            ix = 5 + WIND.index(Dv)
                            ex.append((C.jrev, C.jrev[:], hk, hk[:, 2 * ix, :]))
                            ex.append((C.jrev, C.jrev[:], hk, hk[:, 2 * ix + 1, :]))
                        dst = C.o_sel if br == 0 else C.o_win
                        yield dict(q=(q, qa), k=(k, ka), extras=ex, bias=bias,
                                   v=(v, v[:, kt, br, :]), first=ti == 0, last=ti == len(kts) - 1,
                                   done=lambda O, qc=qc, gh=gh, dst=dst: evac_norm(fw, C, O, R.spb, 64, dst, qc, gh * 64, otr, rzr))
                        load(gh + 1)
                        if gh % 4 == 3:
                            load_group(g + 1)

    attn_jobs(fw, R, jobs(), C)
    fw.phase_end()


def phase_out(fw, C, l, xsrc, last):
    fw.phase_begin()
    wout = fw.sb([128, 8, D], BF16, "wout")
    wos = fw.ring(2, [128, D], F32, "wos")
    gr = fw.ring(3, [128, GW], F32, "g")
    szr = fw.ring(2, [128, GW], F32, "sz")
    sgr = fw.ring(2, [128, 24], F32, "sg")
    oar = fw.ring(3, [128, 256], F32, "oa")
    ofr = fw.ring(3, [128, 256], F32, "of")
    ocr = fw.ring(3, [128, 512], F32, "oc")
    osr = fw.ring(3, [128, 512], F32, "os")
    owr = fw.ring(3, [128, 512], F32, "ow")
    t1r = fw.ring(2, [128, 512], F32, "t1")
    t2r = fw.ring(2, [128, 512], F32, "t2")
    xr = fw.ring(3, [128, D], F32, "xr")
    mixr = fw.ring(2, [128, D], BF16, "mix")
    mTr = fw.ring(2, [128, 8, 128], BF16, "mT")
    xnr = fw.ring(2, [128, D], F32, "xnew")
    ptr = fw.ring(2, [128, 8, 128], BF16, "ptb", psum=True)
    pacc = fw.ring(4, [128, 512], F32, "po", psum=True)
    wo = C.i["w_out"]
    for c in range(8):
        st = wos.next()
        fw.dma(st[:], wo[l, c * 128:(c + 1) * 128, :], writes=[st])
        fw.cp("dve" if c % 2 == 0 else "pool", wout[:, c, :], st[:], reads=[st], pwrites=[wout])
    if last:
        fg = fw.sb([128, D], F32, "fg")
        fw.dma(fg[:], C.i["fing"][0:1, :].broadcast_to([128, D]), writes=[fg])
        junk = fw.sb([128, D], BF16, "junk2")
        ssr = fw.ring(2, [128, 1], F32, "ss2")
        yr = fw.ring(2, [128, D], F32, "yo")
    def loads(i):
        rows = slice(i * 128, (i + 1) * 128)
        g_, oa, of_, oc, os_, ow, xt = gr.next(), oar.next(), ofr.next(), ocr.next(), osr.next(), owr.next(), xr.next()
        fw.dma(g_[:], C.gates[rows, :], reads=[C.gates], writes=[g_])
        fw.dma(oa[:], C.o_sb[rows, :], reads=[C.o_sb], writes=[oa])
        fw.dma(of_[:], C.o_fox[rows, :], reads=[C.o_fox], writes=[of_])
        fw.dma(oc[:], C.o_cmp[rows, :], reads=[C.o_cmp], writes=[oc])
        fw.dma(os_[:], C.o_sel[rows, :], reads=[C.o_sel], writes=[os_])
        fw.dma(ow[:], C.o_win[rows, :], reads=[C.o_win], writes=[ow])
        fw.dma(xt[:], xsrc[rows, :], reads=[xsrc], writes=[xt])
        return (g_, oa, of_, oc, os_, ow, xt)

    pre = [loads(0), loads(1)]
    for i in range(NT):
        rows = slice(i * 128, (i + 1) * 128)
        g_, oa, of_, oc, os_, ow, xt = pre.pop(0)
        if i + 2 < NT:
            pre.append(loads(i + 2))
        sz, sg = szr.next(), sgr.next()
        fw.act(sz[:], g_[:], AF.Silu, reads=[g_], writes=[sz])
        fw.act(sg[:], g_[:, 256:280], AF.Sigmoid, reads=[g_], writes=[sg])
        mix = mixr.next()
        t1, t2 = t1r.next(), t2r.next()
        sg3 = sg[:].rearrange("p (h b) -> p h b", b=3)

        def bview(ap):
            return ap.rearrange("p (h d) -> p h d", d=64)

        fw.tt("dve", mix[:, 0:256], oa[:], sz[:, 0:256], ALU.mult, reads=[oa, sz], writes=[mix])
        fw.tt("pool", bview(t1[:]), bview(oc[:]), sg3[:, :, 0:1].broadcast_to([128, 8, 64]), ALU.mult, reads=[oc, sg], writes=[t1])
        fw.tt("dve", bview(t2[:]), bview(os_[:]), sg3[:, :, 1:2].broadcast_to([128, 8, 64]), ALU.mult, reads=[os_, sg], writes=[t2])
        fw.tt("pool", t1[:], t1[:], t2[:], ALU.add, reads=[t2], writes=[t1])
        fw.tt("dve", bview(t2[:]), bview(ow[:]), sg3[:, :, 2:3].broadcast_to([128, 8, 64]), ALU.mult, reads=[ow, sg], writes=[t2])
        fw.tt("pool", t1[:], t1[:], t2[:], ALU.add, reads=[t2], writes=[t1])
        fw.tt("dve", mix[:, 256:768], t1[:], sz[:, 280:792], ALU.mult, reads=[t1, sz], pwrites=[mix])
        fw.tt("pool", mix[:, 768:1024], of_[:], sz[:, 792:1048], ALU.mult, reads=[of_, sz], pwrites=[mix])
        pt = ptr.next()
        for c in range(8):
            fw.tr(pt[:, c, :], mix[:, c * 128:(c + 1) * 128], C.ident[:], reads=[mix, C.ident],
                  writes=[pt] if c == 0 else (), pwrites=() if c == 0 else [pt])
        mT = mTr.next()
        fw.cp("dve", mT[:], pt[:], reads=[pt], writes=[mT])
        xn = xnr.next()
        for hf in range(2):
            ps = pacc.next()
            for k in range(8):
                fw.mm(ps[:], mT[:, k, :], wout[:, k, hf * 512:(hf + 1) * 512], k == 0, k == 7, reads=[mT, wout],
                      writes=[ps] if k == 0 else (), pwrites=() if k == 0 else [ps])
            fw.tt("dve", xn[:, hf * 512:(hf + 1) * 512], ps[:], xt[:, hf * 512:(hf + 1) * 512], ALU.add, reads=[ps, xt],
                  writes=[xn] if hf == 0 else (), pwrites=() if hf == 0 else [xn])
        if not last:
            fw.dma(C.xres[rows, :], xn[:], reads=[xn], pwrites=[C.xres])
        else:
            ss = ssr.next()
            fw.memset("pool", ss[:], 0.0, writes=[ss])
            fw.act(junk[:], xn[:], AF.Square, accum_out=ss[:], reads=[xn], writes=[junk, ss])
            fw.act(ss[:], ss[:], AF.Sqrt, scale=1.0 / D, bias=1e-6, writes=[ss])
            fw.op("dve", lambda e, o=ss[:]: e.reciprocal(out=o, in_=o), writes=[ss])
            y = yr.next()
            fw.stt("dve", y[:], xn[:], ss[:, 0:1], fg[:], ALU.mult, ALU.mult, reads=[xn, ss, fg], writes=[y])
            fw.dma(C.out[rows, :], y[:], reads=[y], pwrites=[C.out])
    fw.phase_end()


def build(dbg=(), stop_after=None, nlayers=NL, only=None):
    nc = bass.Bass("TRN2", target_bir_lowering=False)
    fw = FW(nc)
    C = _declare(fw, dbg)
    phase_consts(fw, C)
    for l in range(nlayers):
        xsrc = C.i["x"] if l == 0 else C.xres
        phase_proj(fw, C, l, xsrc)
        if only is None or "fox" in only:
            phase_foxaux(fw, C, l)
            phase_fox(fw, C, l)
        if only is None or "sb" in only:
            phase_sb(fw, C, l)
        if only is None or "nsa" in only:
            phase_compress(fw, C, l)
            phase_cmp(fw, C, l)
            phase_selwin(fw, C, l)
        if only is None or "out" in only:
            phase_out(fw, C, l, xsrc, last=(l == nlayers - 1))
    fw.phase_begin()
    fw.phase_end(final=True)
    fw.close()
    return nc


_NC_CACHE = {}


def kernel(**inputs):
    x = np.ascontiguousarray(np.asarray(inputs["x"], np.float32))
    n = x.shape[0]
    shared = _prep_shared(inputs)
    if "nc" not in _NC_CACHE:
        _NC_CACHE["nc"] = build()
    nc = _NC_CACHE["nc"]
    in_maps = []
    for b in range(n):
        m = dict(shared)
        m["x"] = x[b]
        in_maps.append(m)
    res = run_bass_kernel_spmd(nc, in_maps, core_ids=list(range(n)))
    return np.stack([np.asarray(r["out"], np.float32) for r in res.results], axis=0)
```

```python
import contextlib
import numpy as np
import concourse.bass as bass
import concourse.mybir as mybir

F32 = mybir.dt.float32
BF16 = mybir.dt.bfloat16
AF = mybir.ActivationFunctionType
ALU = mybir.AluOpType
AX = mybir.AxisListType

ENGS = ("pe", "act", "dve", "pool", "sp")
_ENG_ATTR = {"pe": "tensor", "act": "scalar", "dve": "vector", "pool": "gpsimd", "sp": "sync"}


class Buf:
    def __init__(self, t, name, psum=False):
        self.t = t
        self.name = name
        self.psum = psum
        self.w = {}
        self.r = {}
        self.pr = {}
        self.w0 = {}

    def __getitem__(self, k):
        return self.t[k]

    def ap(self):
        return self.t.ap() if hasattr(self.t, "ap") and callable(getattr(self.t, "ap")) else self.t[:]


class Ring:
    def __init__(self, bufs):
        self.bufs = bufs
        self.i = 0

    def next(self):
        b = self.bufs[self.i % len(self.bufs)]
        self.i += 1
        return b


class FW:
    def __init__(self, nc, n_dma_sems=12):
        self.nc = nc
        self.es = contextlib.ExitStack()
        self.pes = None
        self.streams = {e: [] for e in ENGS}
        self.phase = 0
        self.cnt = {}
        self.sems = {}
        self.seen = {e: {} for e in ENGS}
        self.dma_pool = {}
        self.n_dma_sems = n_dma_sems
        self.nbuf = 0
        self.barrier = {}
        self.n_ops = 0
        self.n_waits = 0

    def sem(self, key):
        if key not in self.sems:
            self.sems[key] = self.es.enter_context(self.nc.semaphore("s%d" % len(self.sems)))
        return self.sems[key]

    def sb(self, shape, dt, name=None, glob=False):
        self.nbuf += 1
        name = (name or "sb") + "_%d" % self.nbuf
        es = self.es if (glob or self.pes is None) else self.pes
        t = es.enter_context(self.nc.sbuf_tensor(name, list(shape), dt))
        return Buf(t, name)

    def ps(self, shape, dt, name=None, glob=False):
        self.nbuf += 1
        name = (name or "ps") + "_%d" % self.nbuf
        es = self.es if (glob or self.pes is None) else self.pes
        t = es.enter_context(self.nc.psum_tensor(name, list(shape), dt))
        return Buf(t, name, psum=True)

    def phase_begin(self):
        assert self.pes is None
        self.pes = contextlib.ExitStack()
        self.barrier = self.final_events()
        for e in ENGS:
            for k, v in self.barrier.items():
                if self.seen[e].get(k, 0) < v:
                    self.seen[e][k] = v

    def phase_end(self, final=False):
        self.emit(final=final)
        self.streams = {e: [] for e in ENGS}
        self.pes.close()
        self.pes = None

    def dram(self, name, shape, dt, kind="Internal"):
        t = self.nc.dram_tensor(name, list(shape), dt, kind=kind)
        return Buf(t, name)

    def ring(self, n, shape, dt, name, psum=False):
        return Ring([(self.ps if psum else self.sb)(shape, dt, "%s%d" % (name, i)) for i in range(n)])

    def _need(self, eng, deps, out):
        seen = self.seen[eng]
        for k, v in deps.items():
            if eng == "pe" and k[1] == "pe" and k[0] != "dma":
                continue
            if seen.get(k, 0) < v:
                out[k] = max(out.get(k, 0), v)

    def _track(self, eng, ev, reads, writes, pwrites):
        need = {}
        for b in reads:
            self._need(eng, b.w, need)
            if b.psum:
                self._need(eng, {k: v for k, v in b.r.items() if k[1] != eng}, need)
        for b in writes:
            self._need(eng, b.w, need)
            self._need(eng, b.r, need)
        for b in pwrites:
            self._need(eng, b.r, need)
            self._need(eng, b.pr, need)
            self._need(eng, b.w0, need)
        for k, v in need.items():
            self.seen[eng][k] = v
        k, v = ev
        for b in reads:
            b.r[k] = max(b.r.get(k, 0), v)
        for b in writes:
            pr = dict(b.w)
            for kk, vv in b.r.items():
                pr[kk] = max(pr.get(kk, 0), vv)
            b.pr = pr
            b.w = {k: v}
            b.w0 = {k: v}
            b.r = {}
        for b in pwrites:
            b.w[k] = max(b.w.get(k, 0), v)
        self.n_waits += len(need)
        return list(need.items())

    def op(self, eng, fn, reads=(), writes=(), pwrites=()):
        key = (self.phase, eng)
        self.cnt[key] = self.cnt.get(key, 0) + 1
        ev = (key, self.cnt[key])
        waits = self._track(eng, ev, reads, writes, pwrites)
        self.sem(key)
        self.streams[eng].append((waits, fn, key, 1))
        self.n_ops += 1

    def dma(self, out, in_, reads=(), writes=(), pwrites=(), q="sp", **kw):
        pool = self.dma_pool.setdefault(q, {"i": 0, "tot": [0] * self.n_dma_sems})
        i = pool["i"] % self.n_dma_sems
        pool["i"] += 1
        key = ("dma", q, i)
        prev = pool["tot"][i]
        pool["tot"][i] = prev + 16
        ev = (key, prev + 16)
        waits = self._track(q, ev, reads, writes, pwrites)
        if prev > 0 and self.seen[q].get(key, 0) < prev:
            waits.append((key, prev))
            self.seen[q][key] = prev
        self.sem(key)
        self.streams[q].append((waits, lambda e: e.dma_start(out=out, in_=in_, **kw), key, 16))
        self.n_ops += 1

    def mm(self, out, lhsT, rhs, start, stop, reads=(), writes=(), pwrites=()):
        self.op("pe", lambda e: e.matmul(out, lhsT=lhsT, rhs=rhs, start=start, stop=stop), reads, writes, pwrites)

    def tr(self, out, in_, ident, reads=(), writes=(), pwrites=()):
        self.op("pe", lambda e: e.transpose(out=out, in_=in_, identity=ident), reads, writes, pwrites)

    def act(self, out, in_, func, reads=(), writes=(), pwrites=(), **kw):
        self.op("act", lambda e: e.activation(out=out, in_=in_, func=func, **kw), reads, writes, pwrites)

    def ts(self, eng, out, in0, s1, s2, op0, op1=None, reads=(), writes=(), pwrites=()):
        if op1 is None:
            self.op(eng, lambda e: e.tensor_scalar(out=out, in0=in0, scalar1=s1, scalar2=s2, op0=op0), reads, writes, pwrites)
        else:
            self.op(eng, lambda e: e.tensor_scalar(out=out, in0=in0, scalar1=s1, scalar2=s2, op0=op0, op1=op1), reads, writes, pwrites)

    def tt(self, eng, out, in0, in1, op, reads=(), writes=(), pwrites=()):
        self.op(eng, lambda e: e.tensor_tensor(out=out, in0=in0, in1=in1, op=op), reads, writes, pwrites)

    def stt(self, eng, out, in0, scalar, in1, op0, op1, reads=(), writes=(), pwrites=()):
        self.op(eng, lambda e: e.scalar_tensor_tensor(out=out, in0=in0, scalar=scalar, in1=in1, op0=op0, op1=op1), reads, writes, pwrites)

    def cp(self, eng, out, in_, reads=(), writes=(), pwrites=()):
        if eng == "act":
            self.op(eng, lambda e: e.copy(out=out, in_=in_), reads, writes, pwrites)
        else:
            self.op(eng, lambda e: e.tensor_copy(out=out, in_=in_), reads, writes, pwrites)

    def memset(self, eng, ap, val, writes=(), pwrites=()):
        self.op(eng, lambda e: e.memset(ap, val), (), writes, pwrites)

    def final_events(self):
        ev = dict(self.cnt)
        for q, pool in self.dma_pool.items():
            for i, tot in enumerate(pool["tot"]):
                if tot:
                    ev[("dma", q, i)] = tot
        return ev

    def emit(self, final=True):
        nc = self.nc
        fin = self.final_events() if final else {}
        with nc.Block() as block:
            bar = dict(self.barrier)

            def run(e, name):
                for k, v in bar.items():
                    e.wait_ge(self.sems[k], v)
                for waits, fn, key, amt in self.streams[name]:
                    for k, v in waits:
                        e.wait_ge(self.sems[k], v)
                    fn(e).then_inc(self.sems[key], amt)
                if name == "sp":
                    for k, v in fin.items():
                        e.wait_ge(self.sems[k], v)

            @block.tensor
            def _(e):
                run(e, "pe")

            @block.scalar
            def _(e):
                run(e, "act")

            @block.vector
            def _(e):
                run(e, "dve")

            @block.gpsimd
            def _(e):
                run(e, "pool")

            @block.sync
            def _(e):
                run(e, "sp")

    def close(self):
        self.es.close()


import math
import ml_dtypes
from concourse.bass_utils import run_bass_kernel_spmd

T, D, IW, NT, NQC = 4096, 1024, 3868, 32, 8
NB, OFF = 4608, 2064
NEG = -30000.0
NL = 2
NPBF = ml_dtypes.bfloat16
VW = 12 * 65
GW = 1048


class NS:
    pass


import os as _os
TMODE_FOX = bool(int(_os.environ.get("K_TMODE_FOX", "1")))
TMODE_SEL = bool(int(_os.environ.get("K_TMODE_SEL", "0")))


def _bucket(n):
    n = np.maximum(n, 0)
    nf = np.maximum(n, 1).astype(np.float32)
    large = 16 + (np.log(nf / np.float32(16)) / np.float32(math.log(128 / 16)) * np.float32(16)).astype(np.int32)
    large = np.minimum(large, 31)
    return np.where(n < 16, n, large)


def _consts():
    c = {}
    eye = np.eye(128, dtype=np.float32)
    c["c_ident"] = eye.astype(NPBF)
    c["c_identf"] = eye.copy()
    c["c_jrev"] = eye[::-1].copy().astype(NPBF)
    c["c_negi"] = (-eye).astype(NPBF)
    j = np.arange(128)[:, None]
    s = np.arange(128)[None, :]
    c["c_tri"] = (j >= s).astype(np.float32).astype(NPBF)
    c["c_ones"] = np.ones((128, 128), np.float32).astype(NPBF)
    sl = np.arange(128)[:, None, None]
    r = np.arange(4)[None, :, None]
    i = np.arange(512)[None, None, :]
    c["c_msb"] = np.where(128 * r + sl >= i, NEG, 0.0).astype(np.float32).astype(NPBF)
    c["c_mfox"] = np.where(128 * r + sl > i, NEG, 0.0).astype(np.float32).astype(NPBF)
    jj = np.arange(64)[:, None, None]
    kt = np.arange(32)[None, :, None]
    k = np.arange(128)[None, None, :]
    c["c_eexp"] = np.where((128 * kt + k) // 64 == jj, 30000.0, 0.0).astype(np.float32).astype(NPBF)
    n = np.arange(NB) - OFF
    b = _bucket(n)
    oh = np.zeros((32, NB), np.float32)
    oh[b[n >= 0], np.nonzero(n >= 0)[0]] = 1.0
    c["c_oh"] = oh
    mrow = np.zeros((8, 2, NB), np.float32)
    mrow[:, 0, n < 0] = NEG
    mrow[:, 1, (n < 0) | (n >= 512)] = NEG
    c["c_mrow"] = mrow
    t = np.arange(T)[:, None]
    jb = np.arange(64)[None, :]
    cur = t // 64
    F = np.zeros((T, 64), np.float32)
    F[(jb == 0) | (jb == cur) | (jb == cur - 1)] = 1e4
    F[np.broadcast_to(jb > cur, F.shape)] = -1e4
    c["c_fst"] = F
    cc = np.arange(256)[:, None]
    cend = cc * 16 + 31
    cstart = cc * 16
    ov = np.clip(np.minimum(cend, jb * 64 + 63) - np.maximum(cstart, jb * 64) + 1, 0, None).astype(np.float32) / 32.0
    ov[255] = 0.0
    c["c_ovl"] = np.ascontiguousarray(ov.reshape(2, 128, 64).transpose(1, 0, 2)).astype(NPBF)
    return c


def _prep_shared(inp):
    m = {}
    f = np.float32
    m["w_in"] = np.ascontiguousarray(inp["w_in"], f)
    m["w_out"] = np.ascontiguousarray(inp["w_out"], f)
    m["g_in"] = np.ascontiguousarray(np.asarray(inp["norm_g"], f).reshape(NL, 8, 128).transpose(2, 0, 1))
    m["fb"] = np.ascontiguousarray(np.asarray(inp["forget_b"], f).T)
    m["w1"] = np.ascontiguousarray(inp["cmp_w1"], f)
    m["b1p"] = np.ascontiguousarray(np.asarray(inp["cmp_b1"], f).reshape(NL, 2, 2, 128).transpose(3, 0, 1, 2))
    m["w2"] = np.ascontiguousarray(inp["cmp_w2"], f)
    m["peT"] = np.ascontiguousarray(np.asarray(inp["cmp_pe"], f).transpose(3, 0, 1, 2))
    m["relb"] = np.ascontiguousarray(inp["rel_bias"], f)
    m["fing"] = np.ascontiguousarray(np.asarray(inp["final_g"], f).reshape(1, D))
    m.update(_consts())
    return m


_IN_SPECS = [
    ("x", [T, D], F32), ("w_in", [NL, D, IW], F32), ("w_out", [NL, D, D], F32), ("g_in", [128, NL, 8], F32),
    ("fb", [4, NL], F32), ("w1", [NL, 2, 2048, 256], F32), ("b1p", [128, NL, 2, 2], F32),
    ("w2", [NL, 2, 256, 64], F32), ("peT", [64, NL, 2, 32], F32), ("relb", [32, 8], F32), ("fing", [1, D], F32),
    ("c_ident", [128, 128], BF16), ("c_identf", [128, 128], F32), ("c_jrev", [128, 128], BF16), ("c_negi", [128, 128], BF16),
    ("c_tri", [128, 128], BF16), ("c_ones", [128, 128], BF16), ("c_msb", [128, 4, 512], BF16),
    ("c_mfox", [128, 4, 512], BF16), ("c_eexp", [64, 32, 128], BF16), ("c_oh", [32, NB], F32),
    ("c_mrow", [8, 2, NB], F32), ("c_fst", [T, 64], F32), ("c_ovl", [128, 2, 64], BF16),
]


def _declare(fw, dbg):
    C = NS()
    C.i = {}
    for name, shape, dt in _IN_SPECS:
        C.i[name] = fw.dram(name, shape, dt, kind="ExternalInput")
    C.out = fw.dram("out", [T, D], F32, kind="ExternalOutput")

    def scr(name, shape, dt):
        return fw.dram(name, shape, dt, kind="ExternalOutput" if name in dbg else "Internal")

    C.xres = scr("xres", [T, D], F32)
    C.sbQ = scr("sbQ", [256, T], BF16)
    C.sbK = scr("sbK", [256, T], BF16)
    C.sbKn = scr("sbKn", [256, T], BF16)
    C.nsaQ = scr("nsaQ", [512, T], BF16)
    C.cmpKin = scr("cmpKin", [128, T], BF16)
    C.cmpVin = scr("cmpVin", [128, T], BF16)
    C.selK = scr("selK", [128, T], BF16)
    C.winK = scr("winK", [128, T], BF16)
    C.foxQ = scr("foxQ", [4, 70, T], BF16)
    C.foxK = scr("foxK", [4, 70, T], BF16)
    C.fcT = scr("fcT", [4, T], F32)
    C.Vtok = scr("Vtok", [NT, 128, VW], BF16)
    C.gates = scr("gates", [T, GW], F32)
    C.bv = {(v, p): scr("bv%d%s" % (v, p), [8, NB], BF16) for v in (0, 1) for p in ("hi", "lo")}
    C.selT = scr("selT", [2, 64, T], BF16)
    C.o_sb = scr("o_sb", [T, 256], F32)
    C.o_cmp = scr("o_cmp", [T, 512], F32)
    C.o_sel = scr("o_sel", [T, 512], F32)
    C.o_win = scr("o_win", [T, 512], F32)
    C.o_fox = scr("o_fox", [T, 256], F32)
    return C


def phase_consts(fw, C):
    fw.phase_begin()

    def gconst(name, shape, dt):
        b = fw.sb(shape, dt, name, glob=True)
        src = C.i[name]
        fw.dma(b[:], src[:], reads=[src], writes=[b])
        return b

    C.ident = gconst("c_ident", [128, 128], BF16)
    C.identf = gconst("c_identf", [128, 128], F32)
    C.jrev = gconst("c_jrev", [128, 128], BF16)
    C.negi = gconst("c_negi", [128, 128], BF16)
    C.tri = gconst("c_tri", [128, 128], BF16)
    C.ones = gconst("c_ones", [128, 128], BF16)
    C.msb = gconst("c_msb", [128, 4, 512], BF16)
    C.mfox = gconst("c_mfox", [128, 4, 512], BF16)
    C.ovl = gconst("c_ovl", [128, 2, 64], BF16)
    C.g_sb = gconst("g_in", [128, NL, 8], F32)
    C.tab31 = fw.sb([128, 8], F32, "tab31", glob=True)
    fw.dma(C.tab31[:], C.i["relb"][31:32, :].broadcast_to([128, 8]), reads=[C.i["relb"]], writes=[C.tab31])
    C.kcmpT = fw.sb([64, 2, 256], BF16, "kcmpT", glob=True)
    C.vcaug = fw.sb([128, 2, 2, 129], BF16, "vcaug", glob=True)

    relb = fw.sb([32, 8], F32, "relb")
    oh = fw.sb([32, NB], F32, "oh")
    mrow = fw.sb([8, 2, NB], F32, "mrow")
    fw.dma(relb[:], C.i["relb"][:], writes=[relb])
    fw.dma(oh[:], C.i["c_oh"][:], writes=[oh])
    fw.dma(mrow[:], C.i["c_mrow"][:], writes=[mrow])
    pr = fw.ring(2, [128, 512], F32, "p0", psum=True)
    bvf = [fw.sb([8, NB], F32, "bvf%d" % v) for v in (0, 1)]
    for ch in range(NB // 512):
        ps = pr.next()
        sl = slice(ch * 512, (ch + 1) * 512)
        fw.mm(ps[0:8, :], relb[:], oh[:, sl], True, True, reads=[relb, oh], writes=[ps])
        for v in (0, 1):
            fw.tt("dve", bvf[v][:, sl], ps[0:8, :], mrow[:, v, sl], ALU.add, reads=[ps, mrow], pwrites=[bvf[v]])
    for v in (0, 1):
        hi = fw.sb([8, NB], BF16, "bvhi%d" % v)
        lo = fw.sb([8, NB], BF16, "bvlo%d" % v)
        fw.cp("dve", hi[:], bvf[v][:], reads=[bvf[v]], writes=[hi])
        fw.tt("dve", lo[:], bvf[v][:], hi[:], ALU.subtract, reads=[bvf[v], hi], writes=[lo])
        fw.dma(C.bv[(v, "hi")][:], hi[:], reads=[hi], pwrites=[C.bv[(v, "hi")]])
        fw.dma(C.bv[(v, "lo")][:], lo[:], reads=[lo], pwrites=[C.bv[(v, "lo")]])
    fw.phase_end()


def _fm_chunks(C):
    ch = []
    for j in range(2):
        ch.append((0 + 128 * j, 128, 0.125, [(0, 128, C.sbQ, 128 * j, None)]))
    for j in range(2):
        ch.append((256 + 128 * j, 128, 1.0, [(0, 128, C.sbK, 128 * j, C.sbKn)]))
    for j in range(4):
        ch.append((1024 + 128 * j, 128, 0.125, [(0, 128, C.nsaQ, 128 * j, None)]))
    ch.append((1536, 128, 1.0, [(0, 128, C.cmpKin, 0, None)]))
    ch.append((1664, 128, 1.0, [(0, 128, C.cmpVin, 0, None)]))
    ch.append((1792, 128, 1.0, [(0, 128, C.selK, 0, None)]))
    ch.append((2048, 128, 1.0, [(0, 128, C.winK, 0, None)]))
    for j in range(2):
        ch.append((2840 + 128 * j, 128, 0.125, [(0, 64, ("fox", C.foxQ, 2 * j), 0, None), (64, 128, ("fox", C.foxQ, 2 * j + 1), 0, None)]))
    for j in range(2):
        ch.append((3096 + 128 * j, 128, 1.0, [(0, 64, ("fox", C.foxK, 2 * j), 0, None), (64, 128, ("fox", C.foxK, 2 * j + 1), 0, None)]))
    ch.append((3608, 4, 1.0, [(0, 4, ("f32", C.fcT), 0, None)]))
    return ch


def phase_proj(fw, C, l, xsrc, stop=None):
    fw.phase_begin()
    hT = fw.sb([128, 8, T], BF16, "hT")
    W = fw.sb([128, 8, IW], BF16, "W")
    wst = fw.ring(3, [128, IW // 4], F32, "wst")
    xr = fw.ring(3, [128, D], F32, "xt")
    junk = fw.sb([128, D], BF16, "junk")
    xnr = fw.ring(2, [128, D], BF16, "xn")
    ssr = fw.ring(2, [128, 1], F32, "ss")
    rsr = fw.ring(2, [128, 1], F32, "rs")
    ptr = fw.ring(2, [128, 8, 128], BF16, "pt", psum=True)
    pacc = fw.ring(5, [128, 512], F32, "pa", psum=True)
    fmo = fw.ring(3, [128, 512], BF16, "fmo")
    fmn = fw.ring(2, [128, 512], BF16, "fmn")
    fmf = fw.ring(2, [4, 512], F32, "fmf")
    vst = fw.ring(2, [128, 12, 65], BF16, "vst")
    gst = fw.ring(2, [128, GW], F32, "gst")
    hw = IW // 4
    win = C.i["w_in"]
    wjobs = [(c, hf) for c in range(8) for hf in range(4)]

    def wload(n):
        for _ in range(n):
            if not wjobs:
                return
            c, hf = wjobs.pop(0)
            st = wst.next()
            fw.dma(st[:], win[l, c * 128:(c + 1) * 128, hf * hw:(hf + 1) * hw], reads=[win], writes=[st])
            fw.ts("pool" if hf % 2 else "dve", W[:, c, hf * hw:(hf + 1) * hw], st[:], C.g_sb[:, l, c:c + 1], None, ALU.mult,
                  reads=[st, C.g_sb], pwrites=[W])

    for b in vst.bufs:
        fw.memset("pool", b[:], 1.0, writes=[b])
    for i in range(NT):
        xt = xr.next()
        fw.dma(xt[:], xsrc[i * 128:(i + 1) * 128, :], reads=[xsrc], writes=[xt])
        wload(1)
        ss = ssr.next()
        rs = rsr.next()
        fw.memset("pool", ss[:], 0.0, writes=[ss])
        fw.act(junk[:], xt[:], AF.Square, accum_out=ss[:], reads=[xt], writes=[junk, ss])
        fw.act(rs[:], ss[:], AF.Sqrt, scale=1.0 / D, bias=1e-6, reads=[ss], writes=[rs])
        fw.op("dve", lambda e, o=rs[:]: e.reciprocal(out=o, in_=o), reads=[rs], writes=[rs])
        xn = xnr.next()
        fw.ts("dve", xn[:], xt[:], rs[:, 0:1], None, ALU.mult, reads=[xt, rs], writes=[xn])
        pt = ptr.next()
        for c in range(8):
            fw.tr(pt[:, c, :], xn[:, c * 128:(c + 1) * 128], C.ident[:], reads=[xn, C.ident],
                  writes=[pt] if c == 0 else (), pwrites=() if c == 0 else [pt])
        fw.cp("dve", hT[:, :, i * 128:(i + 1) * 128], pt[:], reads=[pt], pwrites=[hT])
    wload(100)
    if stop == 2:
        fw.phase_end()
        return
    ev = 0
    import os
    chunks = _fm_chunks(C)
    if os.environ.get("K_NOFC"):
        chunks = chunks[:-1]
    if os.environ.get("K_FEW"):
        chunks = chunks[:int(os.environ["K_FEW"])]
    for (c0, M, scale, dests) in chunks:
        for qc in range(NQC):
            ps = pacc.next()
            tsl = slice(qc * 512, (qc + 1) * 512)
            for k in range(8):
                fw.mm(ps[0:M, :], W[:, k, c0:c0 + M], hT[:, k, tsl], k == 0, k == 7, reads=[W, hT],
                      writes=[ps] if k == 0 else (), pwrites=() if k == 0 else [ps])
            if isinstance(dests[0][2], tuple) and dests[0][2][0] == "f32":
                o = fmf.next()
                fw.cp("dve", o[:], ps[0:M, :], reads=[ps], writes=[o])
                dd = dests[0][2][1]
                fw.dma(dd[:, tsl], o[:], reads=[o], pwrites=[dd])
                continue
            o = fmo.next()
            eng = "dve" if (ev % 2 == 0 or os.environ.get("K_NOACT")) else "act"
            ev += 1
            if eng == "act":
                fw.op("act", lambda e, oo=o[0:M, :], ii=ps[0:M, :], sc=scale: e.mul(out=oo, in_=ii, mul=sc), reads=[ps], writes=[o])
            else:
                fw.ts("dve", o[0:M, :], ps[0:M, :], scale, None, ALU.mult, reads=[ps], writes=[o])
            for (r0, r1, dd, drow, nd) in dests:
                if isinstance(dd, tuple):
                    _, dbuf, hh = dd
                    fw.dma(dbuf[hh, 0:64, tsl], o[r0:r1, :], reads=[o], pwrites=[dbuf])
                else:
                    fw.dma(dd[drow + r0:drow + r1, tsl], o[r0:r1, :], reads=[o], pwrites=[dd])
                if nd is not None:
                    o2 = fmn.next()
                    fw.ts("dve", o2[:], ps[0:M, :], -1.0, None, ALU.mult, reads=[ps], writes=[o2])
                    fw.dma(nd[drow + r0:drow + r1, tsl], o2[r0:r1, :], reads=[o2], pwrites=[nd])
    if stop == 3:
        fw.phase_end()
        return
    groups = [
        (512, 512, [(0, 256, "v", 0), (256, 512, "g", 0)]),
        (1920, 128, [(0, 128, "v", 4)]),
        (2176, 512, [(0, 128, "v", 6), (128, 512, "g", 256)]),
        (2688, 152, [(0, 152, "g", 640)]),
        (3352, 256, [(0, 256, "v", 8)]),
        (3612, 256, [(0, 256, "g", 792)]),
    ]
    for i in range(NT):
        vs = vst.next()
        gs = gst.next()
        first_v = True
        first_g = True
        for (c0, N, outs) in groups:
            ps = pacc.next()
            for k in range(8):
                fw.mm(ps[:, 0:N], hT[:, k, i * 128:(i + 1) * 128], W[:, k, c0:c0 + N], k == 0, k == 7, reads=[W, hT],
                      writes=[ps] if k == 0 else (), pwrites=() if k == 0 else [ps])
            for (a, b, kind, dcol) in outs:
                eng = "dve" if ev % 2 == 0 else "act"
                ev += 1
                if kind == "v":
                    nh = (b - a) // 64
                    src = ps[:, a:b].rearrange("p (h d) -> p h d", d=64)
                    dst = vs[:, dcol:dcol + nh, 0:64]
                    wr = dict(writes=[vs]) if first_v else dict(pwrites=[vs])
                    first_v = False
                else:
                    src = ps[:, a:b]
                    dst = gs[:, dcol:dcol + (b - a)]
                    wr = dict(writes=[gs]) if first_g else dict(pwrites=[gs])
                    first_g = False
                fw.cp(eng, dst, src, reads=[ps], **wr)
        fw.dma(C.Vtok[i, :, :], vs[:].rearrange("p h d -> p (h d)"), reads=[vs], pwrites=[C.Vtok])
        fw.dma(C.gates[i * 128:(i + 1) * 128, :], gs[:], reads=[gs], pwrites=[C.gates])
    fw.phase_end()


def phase_foxaux(fw, C, l):
    fw.phase_begin()
    fc = fw.sb([4, T], F32, "fc")
    fbt = fw.sb([4, NL], F32, "fbt")
    nb = fw.sb([4, 1], F32, "nb")
    fw.dma(fc[:], C.fcT[:], reads=[C.fcT], writes=[fc])
    fw.dma(fbt[:], C.i["fb"][:], writes=[fbt])
    fw.ts("dve", nb[:], fbt[:, l:l + 1], -1.0, None, ALU.mult, reads=[fbt], writes=[nb])
    ones = fw.sb([4, T], F32, "ones4")
    cs = fw.sb([4, T], F32, "cs")
    r1 = fw.sb([4, T], F32, "r1")
    AQ = fw.sb([4, 6, T], BF16, "augq")
    AK = fw.sb([4, 6, T], BF16, "augk")
    fw.act(fc[:], fc[:], AF.Exp, scale=-1.0, bias=nb[:, 0:1], reads=[nb], writes=[fc])
    fw.act(fc[:], fc[:], AF.Ln, bias=1.0, writes=[fc])
    fw.memset("pool", ones[:], 1.0, writes=[ones])
    fw.op("dve", lambda e: e.tensor_tensor_scan(out=cs[:], data0=ones[:], data1=fc[:], initial=0.0,
                                                op0=ALU.mult, op1=ALU.add), reads=[ones, fc], writes=[cs])
    fw.memset("pool", AQ[:, 3:6, :], 1.0, writes=[AQ])
    fw.memset("pool", AK[:, 0:3, :], 1.0, writes=[AK])
    fw.cp("dve", AK[:, 3, :], cs[:], reads=[cs], pwrites=[AK])
    fw.tt("dve", r1[:], cs[:], AK[:, 3, :], ALU.subtract, reads=[cs, AK], writes=[r1])
    fw.cp("dve", AK[:, 4, :], r1[:], reads=[r1], pwrites=[AK])
    fw.tt("dve", r1[:], r1[:], AK[:, 4, :], ALU.subtract, reads=[AK], writes=[r1])
    fw.cp("dve", AK[:, 5, :], r1[:], reads=[r1], pwrites=[AK])
    fw.ts("dve", AQ[:, 0:3, :], AK[:, 3:6, :], -1.0, None, ALU.mult, reads=[AK], pwrites=[AQ])
    fw.dma(C.foxQ[:, 64:70, :], AQ[:], reads=[AQ], pwrites=[C.foxQ])
    fw.dma(C.foxK[:, 64:70, :], AK[:], reads=[AK], pwrites=[C.foxK])
    fw.phase_end()


class AttnRes:
    def __init__(self, fw, nv, tmode=False):
        self.nv = nv
        self.tmode = tmode
        if tmode:
            self.OT = fw.ring(2, [128, 512], F32, "OT", psum=True)
            self.otc = fw.ring(2, [128, 512], F32, "otc")
        self.spb = 4 if nv * 4 * 4 <= 2048 else 2
        self.nob = 4 // self.spb
        self.S = fw.ring(3, [128, 512], F32, "S", psum=True)
        self.O = [[fw.ps([128, self.spb, nv], F32, "O%d_%d" % (a, b)) for b in range(self.nob)] for a in range(2)]
        self.oi = 0
        self.P = fw.ring(3, [128, 512], BF16, "P")

    def next_O(self):
        o = self.O[self.oi % 2]
        self.oi += 1
        return o


def attn_jobs(fw, R, jobs, C=None):
    state = {"O": None}
    C_identf_buf = C.identf if C is not None else None
    C_identf = C.identf if C is not None else None

    def emit_S(j):
        S = R.S.next()
        n = 1 + len(j["extras"])
        qb, qa = j["q"]
        kb, ka = j["k"]
        fw.mm(S[:], ka, qa, True, n == 1, reads=[kb, qb], writes=[S])
        for idx, (lb, la, rb, ra) in enumerate(j["extras"]):
            fw.mm(S[:], la, ra, False, idx == n - 2, reads=[lb, rb], pwrites=[S])
        P = R.P.next()
        if j["bias"] is None:
            fw.act(P[:], S[:], AF.Exp, reads=[S], writes=[P])
        else:
            bb, ba = j["bias"]
            fw.act(P[:], S[:], AF.Exp, bias=ba, reads=[S, bb], writes=[P])
        return P

    pending = []

    def emit_PV(j, P):
        vb, va = j["v"]
        if R.tmode:
            if j["first"]:
                state["OT"] = R.OT.next()
            OT = state["OT"]
            fw.mm(OT[0:R.nv, :], va, P[:], j["first"], j["last"], reads=[vb, P],
                  writes=[OT] if j["first"] else (), pwrites=() if j["first"] else [OT])
            if j["last"]:
                otc = R.otc.next()
                fw.cp("act", otc[0:R.nv, :], OT[0:R.nv, :], reads=[OT], writes=[otc])

                def fin(j=j, otc=otc):
                    O = R.next_O()
                    for sl in range(4):
                        fw.tr(O[0][:, sl, :], otc[0:R.nv, sl * 128:(sl + 1) * 128], C_identf[0:R.nv, 0:R.nv],
                              reads=[otc, C_identf_buf], writes=[O[0]] if sl == 0 else (), pwrites=() if sl == 0 else [O[0]])
                    j["done"](O)
                pending.append(fin)
            return
        if j["first"]:
            state["O"] = R.next_O()
        O = state["O"]
        for sl in range(4):
            ob = O[sl // R.spb]
            first_in_bank = j["first"] and (sl % R.spb == 0)
            fw.op("pe", lambda e, o=ob[:, sl % R.spb, :], l=P[:, sl * 128:(sl + 1) * 128], r=va, st=first_in_bank:
                  e.matmul(o, lhsT=l, rhs=r, start=st, stop=False, skip_group_check=True),
                  reads=[P, vb], writes=[ob] if first_in_bank else (), pwrites=() if first_in_bank else [ob])
        if j["last"]:
            j["done"](O)

    prev = None
    for j in jobs:
        P = emit_S(j)
        while pending:
            pending.pop(0)()
        if prev is not None:
            emit_PV(*prev)
        prev = (j, P)
    if prev is not None:
        emit_PV(*prev)
    while pending:
        pending.pop(0)()


def evac_norm(fw, C, O, spb, nv_out, dst, qc, col0, tmp_ring, rz_ring, extra=None):
    ot = tmp_ring.next()
    rz = rz_ring.next()
    for sl in range(4):
        ob = O[sl // spb]
        oo = ob[:, sl % spb, :]
        fw.ts("dve", rz[:, sl:sl + 1], oo[:, 64:65], 1e-30, None, ALU.max, reads=[ob], writes=[rz] if sl == 0 else (),
              pwrites=() if sl == 0 else [rz])
    fw.op("dve", lambda e, o=rz[:]: e.reciprocal(out=o, in_=o), reads=[rz], writes=[rz])
    for sl in range(4):
        ob = O[sl // spb]
        oo = ob[:, sl % spb, :]
        fw.ts("dve", ot[:, sl, :], oo[:, 0:64], rz[:, sl:sl + 1], None, ALU.mult, reads=[ob, rz],
              writes=[ot] if sl == 0 else (), pwrites=() if sl == 0 else [ot])
        if extra is not None:
            extra(sl, ob, oo, rz)
    fw.dma(dst[qc * 512:(qc + 1) * 512, col0:col0 + 64].rearrange("(j p) d -> p j d", p=128), ot[:], reads=[ot], pwrites=[dst])


def load_v(fw, C, ring, h0, nh):
    v = ring.next()
    fw.dma(v[:], C.Vtok[:, :, h0 * 65:(h0 + nh) * 65].rearrange("i p w -> p i w"), reads=[C.Vtok], writes=[v])
    return v


def phase_fox(fw, C, l):
    fw.phase_begin()
    R = AttnRes(fw, 65, tmode=TMODE_FOX)
    qr = fw.ring(2, [70, T], BF16, "fq")
    kr = fw.ring(2, [70, T], BF16, "fk")
    vr = fw.ring(2, [128, NT, 65], BF16, "fv")
    otr = fw.ring(2, [128, 4, 64], F32, "fot")
    rzr = fw.ring(2, [128, 4], F32, "frz")
    loaded = {}

    def load(h):
        if h >= 4 or h in loaded:
            return
        q = qr.next()
        k = kr.next()
        fw.dma(q[:], C.foxQ[h, :, :], reads=[C.foxQ], writes=[q])
        fw.dma(k[:], C.foxK[h, :, :], reads=[C.foxK], writes=[k])
        v = load_v(fw, C, vr, 8 + h, 1)
        loaded[h] = (q, k, v)

    def jobs():
        for h in range(4):
            load(h)
            q, k, v = loaded[h]
            for qc in range(NQC):
                nk = 4 * qc + 4
                for kt in range(nk):
                    ex = []
                    if kt >= 4 * qc:
                        ex.append((C.ident, C.ident[:], C.mfox, C.mfox[:, kt - 4 * qc, :]))
                    yield dict(q=(q, q[:, qc * 512:(qc + 1) * 512]), k=(k, k[:, kt * 128:(kt + 1) * 128]), extras=ex, bias=None,
                               v=(v, v[:, kt, :]), first=kt == 0, last=kt == nk - 1,
                               done=lambda O, qc=qc, h=h: evac_norm(fw, C, O, R.spb, 64, C.o_fox, qc, h * 64, otr, rzr))
                    load(h + 1)

    attn_jobs(fw, R, jobs(), C)
    fw.phase_end()


def phase_sb(fw, C, l):
    fw.phase_begin()
    Zr = fw.ring(2, [128, 512], F32, "Z", psum=True)
    Rr = fw.ring(2, [128, 512], F32, "Rk", psum=True)
    Or = fw.ring(2, [128, 4, 64], F32, "Osb", psum=True)
    qr = fw.ring(2, [64, T], BF16, "sq")
    kr = fw.ring(2, [64, T], BF16, "sk")
    knr = fw.ring(2, [64, T], BF16, "skn")
    vr = fw.ring(2, [128, NT, 65], BF16, "sv")
    e1r = fw.ring(2, [128, 512], F32, "e1")
    spr = fw.ring(3, [128, 512], BF16, "spb")
    wr = fw.ring(3, [128, 512], BF16, "wt")
    accr = fw.ring(2, [128, 512], F32, "acc")
    abr = fw.ring(2, [128, 512], BF16, "accb")
    otr = fw.ring(2, [128, 4, 64], F32, "sot")
    loaded = {}

    def load(h):
        if h >= 4 or h in loaded:
            return
        q, k, kn = qr.next(), kr.next(), knr.next()
        fw.dma(q[:], C.sbQ[h * 64:(h + 1) * 64, :], reads=[C.sbQ], writes=[q])
        fw.dma(k[:], C.sbK[h * 64:(h + 1) * 64, :], reads=[C.sbK], writes=[k])
        fw.dma(kn[:], C.sbKn[h * 64:(h + 1) * 64, :], reads=[C.sbKn], writes=[kn])
        v = load_v(fw, C, vr, h, 1)
        loaded[h] = (q, k, kn, v)

    import os
    NH = int(os.environ.get("K_SBHEADS", 4))

    def tiles():
        for h in range(NH):
            load(h)
            for qc in range(NQC):
                kts = list(range(4 * qc + 3, -1, -1))
                for idx, kt in enumerate(kts):
                    yield (h, qc, idx, kt, idx == len(kts) - 1)
                    load(h + 1)

    def stage_a(tl):
        h, qc, idx, kt, last = tl
        q, k, kn, v = loaded[h]
        r = kt - 4 * qc
        Z = Zr.next()
        qa = q[:, qc * 512:(qc + 1) * 512]
        fw.mm(Z[:], k[:, kt * 128:(kt + 1) * 128], qa, True, r < 0, reads=[k, q], writes=[Z])
        if r >= 0:
            fw.mm(Z[:], C.ident[:], C.msb[:, r, :], False, True, reads=[C.ident, C.msb], pwrites=[Z])
        e1 = e1r.next()
        fw.act(e1[:], Z[:], AF.Exp, reads=[Z], writes=[e1])
        sp = spr.next()
        fw.act(sp[:], e1[:], AF.Ln, bias=1.0, reads=[e1], writes=[sp])
        return sp

    st = {"acc": None, "accb": None, "O": None}

    def stage_b(tl, sp):
        h, qc, idx, kt, last = tl
        q, k, kn, v = loaded[h]
        r = kt - 4 * qc
        qa = q[:, qc * 512:(qc + 1) * 512]
        Rk = Rr.next()
        fw.mm(Rk[:], C.tri[:], sp[:], True, False, reads=[C.tri, sp], writes=[Rk])
        if idx > 0:
            fw.mm(Rk[:], C.ones[:], st["accb"][:], False, False, reads=[C.ones, st["accb"]], pwrites=[Rk])
        fw.mm(Rk[:], kn[:, kt * 128:(kt + 1) * 128], qa, False, r < 0, reads=[kn, q], pwrites=[Rk])
        if r >= 0:
            fw.mm(Rk[:], C.negi[:], C.msb[:, r, :], False, True, reads=[C.negi, C.msb], pwrites=[Rk])
        w = wr.next()
        fw.act(w[:], Rk[:], AF.Exp, scale=-1.0, reads=[Rk], writes=[w])
        if idx == 0:
            st["O"] = Or.next()
        O = st["O"]
        for sl in range(4):
            fb = idx == 0 and sl == 0
            fw.op("pe", lambda e, o=O[:, sl, :], lt=w[:, sl * 128:(sl + 1) * 128], rr=v[:, kt, 0:64], s0=fb:
                  e.matmul(o, lhsT=lt, rhs=rr, start=s0, stop=False, skip_group_check=True),
                  reads=[w, v], writes=[O] if fb else (), pwrites=() if fb else [O])
        if last:
            ot = otr.next()
            fw.cp("dve", ot[:], O[:], reads=[O], writes=[ot])
            fw.dma(C.o_sb[qc * 512:(qc + 1) * 512, h * 64:(h + 1) * 64].rearrange("(j p) d -> p j d", p=128), ot[:],
                   reads=[ot], pwrites=[C.o_sb])
        else:
            if idx == 0:
                acc = accr.next()
                fw.cp("pool", acc[:], sp[:], reads=[sp], writes=[acc])
            else:
                acc = st["acc"]
                fw.tt("pool", acc[:], acc[:], sp[:], ALU.add, reads=[sp], writes=[acc])
            st["acc"] = acc
            ab = abr.next()
            fw.cp("dve", ab[:], acc[:], reads=[acc], writes=[ab])
            st["accb"] = ab

    prev = None
    for tl in tiles():
        sp = stage_a(tl)
        if prev is not None:
            stage_b(*prev)
        prev = (tl, sp)
    stage_b(*prev)
    fw.phase_end()


def phase_compress(fw, C, l):
    fw.phase_begin()
    w1 = fw.sb([64, 32, 256], BF16, "w1")
    w1s = fw.ring(2, [64, 8, 256], F32, "w1s")
    w2 = fw.sb([128, 2, 64], BF16, "w2")
    w2s = fw.sb([128, 2, 64], F32, "w2s")
    pes = fw.sb([64, 32], F32, "pes")
    peb = fw.sb([64, 32], BF16, "peb")
    b1 = fw.sb([128, 2], F32, "b1")
    cb = fw.sb([128, 2], F32, "cb")
    xin = fw.ring(2, [64, T], BF16, "xin")
    hs = fw.ring(2, [128, 2, 256], BF16, "hs")
    pr = fw.ring(3, [128, 512], F32, "pc", psum=True)
    fw.memset("pool", C.vcaug[:], 1.0, writes=[C.vcaug])
    fw.memset("pool", C.kcmpT[:], 0.0, writes=[C.kcmpT])
    for g in range(2):
        fw.cp("pool", C.vcaug[:, g, :, 65:129], C.ovl[:], reads=[C.ovl], pwrites=[C.vcaug])
    for kv in range(2):
        src = C.i["w1"][l, kv].rearrange("(tok d) h -> d tok h", d=64)
        for pc in range(4):
            st = w1s.next()
            fw.dma(st[:], src[:, pc * 8:(pc + 1) * 8, :], writes=[st])
            fw.cp("dve" if pc % 2 == 0 else "pool", w1[:, pc * 8:(pc + 1) * 8, :], st[:], reads=[st],
                  writes=[w1] if pc == 0 else (), pwrites=() if pc == 0 else [w1])
        fw.dma(w2s[:], C.i["w2"][l, kv].rearrange("(hf p) d -> p hf d", p=128), writes=[w2s])
        fw.cp("dve", w2[:], w2s[:], reads=[w2s], writes=[w2])
        fw.dma(pes[:], C.i["peT"][:, l, kv, :], writes=[pes])
        fw.cp("dve", peb[:], pes[:], reads=[pes], writes=[peb])
        fw.dma(b1[:], C.i["b1p"][:, l, kv, :], writes=[b1])
        for hf in range(2):
            ps = pr.next()
            for tok in range(32):
                fw.mm(ps[:, 0:1], w1[:, tok, hf * 128:(hf + 1) * 128], peb[:, tok:tok + 1], tok == 0, tok == 31,
                      reads=[w1, peb], writes=[ps] if tok == 0 else (), pwrites=() if tok == 0 else [ps])
            fw.tt("dve", cb[:, hf:hf + 1], ps[:, 0:1], b1[:, hf:hf + 1], ALU.add, reads=[ps, b1],
                  writes=[cb] if hf == 0 else (), pwrites=() if hf == 0 else [cb])
        srcT = C.cmpKin if kv == 0 else C.cmpVin
        for g in range(2):
            x = xin.next()
            fw.dma(x[:], srcT[g * 64:(g + 1) * 64, :], reads=[srcT], writes=[x])
            xv = x[:].rearrange("d (j t) -> d j t", t=16)
            h_ = hs.next()
            fw.memset("pool", h_[:], 0.0, writes=[h_])
            for hf in range(2):
                ps = pr.next()
                for tok in range(32):
                    rhs = xv[:, 0:255, tok] if tok < 16 else xv[:, 1:256, tok - 16]
                    fw.mm(ps[:, 0:255], w1[:, tok, hf * 128:(hf + 1) * 128], rhs, tok == 0, tok == 31,
                          reads=[w1, x], writes=[ps] if tok == 0 else (), pwrites=() if tok == 0 else [ps])
                fw.act(h_[:, hf, 0:255], ps[:, 0:255], AF.Silu, bias=cb[:, hf:hf + 1], reads=[ps, cb], pwrites=[h_])
            if kv == 0:
                ps = pr.next()
                for hf in range(2):
                    fw.mm(ps[0:64, 0:255], w2[:, hf, :], h_[:, hf, 0:255], hf == 0, hf == 1, reads=[w2, h_],
                          writes=[ps] if hf == 0 else (), pwrites=() if hf == 0 else [ps])
                fw.cp("dve", C.kcmpT[:, g, 0:255], ps[0:64, 0:255], reads=[ps], pwrites=[C.kcmpT])
            else:
                for ct in range(2):
                    ps = pr.next()
                    for hf in range(2):
                        fw.mm(ps[:, 0:64], h_[:, hf, ct * 128:(ct + 1) * 128], w2[:, hf, :], hf == 0, hf == 1, reads=[w2, h_],
                              writes=[ps] if hf == 0 else (), pwrites=() if hf == 0 else [ps])
                    fw.cp("dve", C.vcaug[:, g, ct, 0:64], ps[:, 0:64], reads=[ps], pwrites=[C.vcaug])
    fw.phase_end()


def hankel_load(fw, C, buf, idx, v, part, head, idx0, pstep):
    src = C.bv[(v, part)]
    ap = bass.AP(src.t, head * NB + idx0, [[pstep, 128], [1, 512]])
    fw.dma(buf[:, idx, :], ap, reads=[src], pwrites=[buf])


def phase_cmp(fw, C, l):
    fw.phase_begin()
    R = AttnRes(fw, 129)
    ptp = fw.ps([64, 512], BF16, "ptp")
    qr = fw.ring(2, [64, T], BF16, "cq")
    hkr = fw.ring(2, [128, 18, 512], BF16, "chk")
    otr = fw.ring(2, [128, 4, 64], F32, "cot")
    rzr = fw.ring(2, [128, 4], F32, "crz")
    imp = fw.sb([128, NT, 64], F32, "imp")
    fst = fw.sb([128, NT, 64], F32, "fst")
    vv = fw.ring(2, [128, 64], F32, "vv")
    v2 = fw.ring(2, [128, 64], F32, "v2")
    mx = fw.ring(2, [128, 16], F32, "mx")
    selr = fw.ring(2, [128, 64], BF16, "sel")
    selTs = fw.sb([64, T], BF16, "selTs")
    fw.dma(fst[:], C.i["c_fst"].ap().rearrange("(tt p) j -> p tt j", p=128) if False else
           C.i["c_fst"][:, :].rearrange("(tt p) j -> p tt j", p=128), writes=[fst])
    tiles_of = {}
    for qc in range(NQC):
        tl = []
        if qc <= 4:
            tl.append((0, True))
        else:
            tl.append((0, False))
        if qc >= 4:
            tl.append((1, True))
        tiles_of[qc] = tl
    hk_index = {}
    n = 0
    for qc in range(NQC):
        for (kt, hank) in tiles_of[qc]:
            if hank:
                hk_index[(qc, kt)] = n
                n += 1
    assert n == 9
    loaded = {}

    def load(gh):
        if gh >= 8 or gh in loaded:
            return
        q = qr.next()
        hk = hkr.next()
        fw.dma(q[:], C.nsaQ[gh * 64:(gh + 1) * 64, :], reads=[C.nsaQ], writes=[q])
        first = True
        for (qc, kt), ix in hk_index.items():
            idx0 = OFF + 512 * qc - 2048 * kt - 2063
            for pi, part in enumerate(("hi", "lo")):
                src = C.bv[(0, part)]
                ap = bass.AP(src.t, gh * NB + idx0, [[16, 128], [1, 512]])
                fw.dma(hk[:, 2 * ix + pi, :], ap, reads=[src], writes=[hk] if first else (), pwrites=() if first else [hk])
                first = False
        loaded[gh] = (q, hk)

    def evac(O, g, h, qc):
        gh = 4 * g + h

        def extra(sl, ob, oo, rz):
            tt_ = qc * 4 + sl
            if h == 0:
                fw.ts("dve", imp[:, tt_, :], oo[:, 65:129], rz[:, sl:sl + 1], None, ALU.mult, reads=[ob, rz], pwrites=[imp])
            else:
                fw.stt("dve", imp[:, tt_, :], oo[:, 65:129], rz[:, sl:sl + 1], imp[:, tt_, :], ALU.mult, ALU.add,
                       reads=[ob, rz], pwrites=[imp])
        evac_norm(fw, C, O, R.spb, 64, C.o_cmp, qc, gh * 64, otr, rzr, extra=extra)

    def jobs(g):
        for h in range(4):
            gh = 4 * g + h
            load(gh)
            q, hk = loaded[gh]
            for qc in range(NQC):
                tl = tiles_of[qc]
                for ti, (kt, hank) in enumerate(tl):
                    ex = []
                    bias = None
                    if hank:
                        ix = hk_index[(qc, kt)]
                        ex.append((C.jrev, C.jrev[:], hk, hk[:, 2 * ix, :]))
                        ex.append((C.jrev, C.jrev[:], hk, hk[:, 2 * ix + 1, :]))
                    else:
                        bias = (C.tab31, C.tab31[:, gh:gh + 1])
                    yield dict(q=(q, q[:, qc * 512:(qc + 1) * 512]), k=(C.kcmpT, C.kcmpT[:, g, kt * 128:(kt + 1) * 128]),
                               extras=ex, bias=bias, v=(C.vcaug, C.vcaug[:, g, kt, :]), first=ti == 0, last=ti == len(tl) - 1,
                               done=lambda O, g=g, h=h, qc=qc: evac(O, g, h, qc))
                    load(gh + 1)

    for g in range(2):
        attn_jobs(fw, R, jobs(g))
        for tt_ in range(NT):
            v_ = vv.next()
            fw.tt("dve", v_[:], imp[:, tt_, :], fst[:, tt_, :], ALU.add, reads=[imp, fst], writes=[v_])
            m_ = mx.next()
            w_ = v2.next()
            fw.op("dve", lambda e, o=m_[:, 0:8], i_=v_[:]: e.max(out=o, in_=i_), reads=[v_], writes=[m_])
            fw.op("dve", lambda e, o=w_[:], a=m_[:, 0:8], b=v_[:]: e.match_replace(out=o, in_to_replace=a, in_values=b, imm_value=-1e30),
                  reads=[v_, m_], writes=[w_])
            fw.op("dve", lambda e, o=m_[:, 8:16], i_=w_[:]: e.max(out=o, in_=i_), reads=[w_], pwrites=[m_])
            sl_ = selr.next()
            fw.ts("dve", sl_[:], v_[:], m_[:, 15:16], 1.0, ALU.is_ge, ALU.subtract, reads=[v_, m_], writes=[sl_])
            j4 = tt_ % 4
            fw.tr(ptp[:, j4 * 128:(j4 + 1) * 128], sl_[:], C.ident[:], reads=[sl_, C.ident],
                  writes=[ptp] if j4 == 0 else (), pwrites=() if j4 == 0 else [ptp])
            if j4 == 3:
                fw.cp("dve", selTs[:, (tt_ - 3) * 128:(tt_ + 1) * 128], ptp[:], reads=[ptp], pwrites=[selTs])
        fw.dma(C.selT[g], selTs[:], reads=[selTs], pwrites=[C.selT])
    fw.phase_end()


def phase_selwin(fw, C, l):
    fw.phase_begin()
    R = AttnRes(fw, 65, tmode=TMODE_SEL)
    qr = fw.ring(2, [128, T], BF16, "nq")
    hkr = fw.ring(2, [128, 26, 512], BF16, "nhk")
    gk = fw.ring(2, [128, 2, T], BF16, "gk")
    gv = fw.ring(2, [128, NT, 2, 65], BF16, "gvv")
    otr = fw.ring(2, [128, 4, 64], F32, "not")
    rzr = fw.ring(2, [128, 4], F32, "nrz")
    SELD = [-384, -256, -128, 0, 128]
    WIND = [-384, -256, -128, 0, 128, 256, 384, 512]
    loaded = {}
    gl = {}
    eflat = C.i["c_eexp"][:, :, :].rearrange("j a b -> j (a b)")

    def load_group(g):
        if g >= 2 or g in gl:
            return
        k = gk.next()
        v = gv.next()
        fw.dma(k[0:64, 0, :], C.selK[g * 64:(g + 1) * 64, :], reads=[C.selK], writes=[k])
        fw.dma(k[64:128, 0, :], eflat, pwrites=[k])
        fw.dma(k[0:64, 1, :], C.winK[g * 64:(g + 1) * 64, :], reads=[C.winK], pwrites=[k])
        fw.dma(v[:, :, 0, :], C.Vtok[:, :, (4 + g) * 65:(5 + g) * 65].rearrange("i p w -> p i w"), reads=[C.Vtok], writes=[v])
        fw.dma(v[:, :, 1, :], C.Vtok[:, :, (6 + g) * 65:(7 + g) * 65].rearrange("i p w -> p i w"), reads=[C.Vtok], pwrites=[v])
        gl[g] = (k, v)

    def load(gh):
        if gh >= 8 or gh in loaded:
            return
        q = qr.next()
        hk = hkr.next()
        fw.dma(q[0:64, :], C.nsaQ[gh * 64:(gh + 1) * 64, :], reads=[C.nsaQ], writes=[q])
        fw.dma(q[64:128, :], C.selT[gh // 4], reads=[C.selT], pwrites=[q])
        first = True
        for vi, Ds in ((0, SELD), (1, WIND)):
            for di, Dv in enumerate(Ds):
                ix = (0 if vi == 0 else 5) + di
                for pi, part in enumerate(("hi", "lo")):
                    src = C.bv[(vi, part)]
                    ap = bass.AP(src.t, gh * NB + OFF + Dv - 127, [[1, 128], [1, 512]])
                    fw.dma(hk[:, 2 * ix + pi, :], ap, reads=[src], writes=[hk] if first else (), pwrites=() if first else [hk])
                    first = False
        loaded[gh] = (q, hk)

    def jobs():
        for gh in range(8):
            g = gh // 4
            load_group(g)
            load(gh)
            q, hk = loaded[gh]
            k, v = gl[g]
            for br in range(2):
                for qc in range(NQC):
                    kts = list(range(4 * qc + 4)) if br == 0 else list(range(max(0, 4 * qc - 4), 4 * qc + 4))
                    for ti, kt in enumerate(kts):
                        Dv = 512 * qc - 128 * kt
                        ex = []
                        bias = None
                        if br == 0:
                            qa = q[:, qc * 512:(qc + 1) * 512]
                            ka = k[:, 0, kt * 128:(kt + 1) * 128]
                            if Dv <= 128:
                                ix = SELD.index(Dv)
                                ex.append((C.jrev, C.jrev[:], hk, hk[:, 2 * ix, :]))
                                ex.append((C.jrev, C.jrev[:], hk, hk[:, 2 * ix + 1, :]))
                            else:
                                bias = (C.tab31, C.tab31[:, gh:gh + 1])
                        else:
                            qa = q[0:64, qc * 512:(qc + 1) * 512]
                            ka = k[0:64, 1, kt * 128:(kt + 1) * 128]
                            ix = 5 + WIND.index(Dv)
                            ex.append((C.jrev, C.jrev[:], hk, hk[:, 2 * ix, :]))
                            ex.append((C.jrev, C.jrev[:], hk, hk[:, 2 * ix + 1, :]))
                        dst = C.o_sel if br == 0 else C.o_win
                        yield dict(q=(q, qa), k=(k, ka), extras=ex, bias=bias,
                                   v=(v, v[:, kt, br, :]), first=ti == 0, last=ti == len(kts) - 1,
                                   done=lambda O, qc=qc, gh=gh, dst=dst: evac_norm(fw, C, O, R.spb, 64, dst, qc, gh * 64, otr, rzr))
                        load(gh + 1)
                        if gh % 4 == 3:
                            load_group(g + 1)

    attn_jobs(fw, R, jobs(), C)
    fw.phase_end()


def phase_out(fw, C, l, xsrc, last):
    fw.phase_begin()
    wout = fw.sb([128, 8, D], BF16, "wout")
    wos = fw.ring(2, [128, D], F32, "wos")
    gr = fw.ring(3, [128, GW], F32, "g")
    szr = fw.ring(2, [128, GW], F32, "sz")
    sgr = fw.ring(2, [128, 24], F32, "sg")
    oar = fw.ring(3, [128, 256], F32, "oa")
    ofr = fw.ring(3, [128, 256], F32, "of")
    ocr = fw.ring(3, [128, 512], F32, "oc")
    osr = fw.ring(3, [128, 512], F32, "os")
    owr = fw.ring(3, [128, 512], F32, "ow")
    t1r = fw.ring(2, [128, 512], F32, "t1")
    t2r = fw.ring(2, [128, 512], F32, "t2")
    xr = fw.ring(3, [128, D], F32, "xr")
    mixr = fw.ring(2, [128, D], BF16, "mix")
    mTr = fw.ring(2, [128, 8, 128], BF16, "mT")
    xnr = fw.ring(2, [128, D], F32, "xnew")
    ptr = fw.ring(2, [128, 8, 128], BF16, "ptb", psum=True)
    pacc = fw.ring(4, [128, 512], F32, "po", psum=True)
    wo = C.i["w_out"]
    for c in range(8):
        st = wos.next()
        fw.dma(st[:], wo[l, c * 128:(c + 1) * 128, :], writes=[st])
        fw.cp("dve" if c % 2 == 0 else "pool", wout[:, c, :], st[:], reads=[st], pwrites=[wout])
    if last:
        fg = fw.sb([128, D], F32, "fg")
        fw.dma(fg[:], C.i["fing"][0:1, :].broadcast_to([128, D]), writes=[fg])
        junk = fw.sb([128, D], BF16, "junk2")
        ssr = fw.ring(2, [128, 1], F32, "ss2")
        yr = fw.ring(2, [128, D], F32, "yo")
    def loads(i):
        rows = slice(i * 128, (i + 1) * 128)
        g_, oa, of_, oc, os_, ow, xt = gr.next(), oar.next(), ofr.next(), ocr.next(), osr.next(), owr.next(), xr.next()
        fw.dma(g_[:], C.gates[rows, :], reads=[C.gates], writes=[g_])
        fw.dma(oa[:], C.o_sb[rows, :], reads=[C.o_sb], writes=[oa])
        fw.dma(of_[:], C.o_fox[rows, :], reads=[C.o_fox], writes=[of_])
        fw.dma(oc[:], C.o_cmp[rows, :], reads=[C.o_cmp], writes=[oc])
        fw.dma(os_[:], C.o_sel[rows, :], reads=[C.o_sel], writes=[os_])
        fw.dma(ow[:], C.o_win[rows, :], reads=[C.o_win], writes=[ow])
        fw.dma(xt[:], xsrc[rows, :], reads=[xsrc], writes=[xt])
        return (g_, oa, of_, oc, os_, ow, xt)

    pre = [loads(0), loads(1)]
    for i in range(NT):
        rows = slice(i * 128, (i + 1) * 128)
        g_, oa, of_, oc, os_, ow, xt = pre.pop(0)
        if i + 2 < NT:
            pre.append(loads(i + 2))
        sz, sg = szr.next(), sgr.next()
        fw.act(sz[:], g_[:], AF.Silu, reads=[g_], writes=[sz])
        fw.act(sg[:], g_[:, 256:280], AF.Sigmoid, reads=[g_], writes=[sg])
        mix = mixr.next()
        t1, t2 = t1r.next(), t2r.next()
        sg3 = sg[:].rearrange("p (h b) -> p h b", b=3)

        def bview(ap):
            return ap.rearrange("p (h d) -> p h d", d=64)

        fw.tt("dve", mix[:, 0:256], oa[:], sz[:, 0:256], ALU.mult, reads=[oa, sz], writes=[mix])
        fw.tt("pool", bview(t1[:]), bview(oc[:]), sg3[:, :, 0:1].broadcast_to([128, 8, 64]), ALU.mult, reads=[oc, sg], writes=[t1])
        fw.tt("dve", bview(t2[:]), bview(os_[:]), sg3[:, :, 1:2].broadcast_to([128, 8, 64]), ALU.mult, reads=[os_, sg], writes=[t2])
        fw.tt("pool", t1[:], t1[:], t2[:], ALU.add, reads=[t2], writes=[t1])
        fw.tt("dve", bview(t2[:]), bview(ow[:]), sg3[:, :, 2:3].broadcast_to([128, 8, 64]), ALU.mult, reads=[ow, sg], writes=[t2])
        fw.tt("pool", t1[:], t1[:], t2[:], ALU.add, reads=[t2], writes=[t1])
        fw.tt("dve", mix[:, 256:768], t1[:], sz[:, 280:792], ALU.mult, reads=[t1, sz], pwrites=[mix])
        fw.tt("pool", mix[:, 768:1024], of_[:], sz[:, 792:1048], ALU.mult, reads=[of_, sz], pwrites=[mix])
        pt = ptr.next()
        for c in range(8):
            fw.tr(pt[:, c, :], mix[:, c * 128:(c + 1) * 128], C.ident[:], reads=[mix, C.ident],
                  writes=[pt] if c == 0 else (), pwrites=() if c == 0 else [pt])
        mT = mTr.next()
        fw.cp("dve", mT[:], pt[:], reads=[pt], writes=[mT])
        xn = xnr.next()
        for hf in range(2):
            ps = pacc.next()
            for k in range(8):
                fw.mm(ps[:], mT[:, k, :], wout[:, k, hf * 512:(hf + 1) * 512], k == 0, k == 7, reads=[mT, wout],
                      writes=[ps] if k == 0 else (), pwrites=() if k == 0 else [ps])
            fw.tt("dve", xn[:, hf * 512:(hf + 1) * 512], ps[:], xt[:, hf * 512:(hf + 1) * 512], ALU.add, reads=[ps, xt],
                  writes=[xn] if hf == 0 else (), pwrites=() if hf == 0 else [xn])
        if not last:
            fw.dma(C.xres[rows, :], xn[:], reads=[xn], pwrites=[C.xres])
        else:
            ss = ssr.next()
            fw.memset("pool", ss[:], 0.0, writes=[ss])
            fw.act(junk[:], xn[:], AF.Square, accum_out=ss[:], reads=[xn], writes=[junk, ss])
            fw.act(ss[:], ss[:], AF.Sqrt, scale=1.0 / D, bias=1e-6, writes=[ss])
            fw.op("dve", lambda e, o=ss[:]: e.reciprocal(out=o, in_=o), writes=[ss])
            y = yr.next()
            fw.stt("dve", y[:], xn[:], ss[:, 0:1], fg[:], ALU.mult, ALU.mult, reads=[xn, ss, fg], writes=[y])
            fw.dma(C.out[rows, :], y[:], reads=[y], pwrites=[C.out])
    fw.phase_end()


def build(dbg=(), stop_after=None, nlayers=NL, only=None):
    nc = bass.Bass("TRN2", target_bir_lowering=False)
    fw = FW(nc)
    C = _declare(fw, dbg)
    phase_consts(fw, C)
    for l in range(nlayers):
        xsrc = C.i["x"] if l == 0 else C.xres
        phase_proj(fw, C, l, xsrc)
        if only is None or "fox" in only:
            phase_foxaux(fw, C, l)
            phase_fox(fw, C, l)
        if only is None or "sb" in only:
            phase_sb(fw, C, l)
        if only is None or "nsa" in only:
            phase_compress(fw, C, l)
            phase_cmp(fw, C, l)
            phase_selwin(fw, C, l)
        if only is None or "out" in only:
            phase_out(fw, C, l, xsrc, last=(l == nlayers - 1))
    fw.phase_begin()
    fw.phase_end(final=True)
    fw.close()
    return nc


_NC_CACHE = {}


def kernel(**inputs):
    x = np.ascontiguousarray(np.asarray(inputs["x"], np.float32))
    n = x.shape[0]
    shared = _prep_shared(inputs)
    if "nc" not in _NC_CACHE:
        _NC_CACHE["nc"] = build()
    nc = _NC_CACHE["nc"]
    in_maps = []
    for b in range(n):
        m = dict(shared)
        m["x"] = x[b]
        in_maps.append(m)
    res = run_bass_kernel_spmd(nc, in_maps, core_ids=list(range(n)))
    return np.stack([np.asarray(r["out"], np.float32) for r in res.results], axis=0)
```
